# Optimizing a Trainium2 kernel written in Bass

```python
import math
import jax, jax.numpy as jnp
from jax import lax
import numpy as np

D_MODEL = 1024
BATCH = 4
SEQ = 4096
DEPTH = 4
DEC_BATCH = 128
DEC_SEQ = 4
PAST_LEN = 8192
PAGE_SIZE = 128

N_MIXERS = 2
N_ATTN_LAYERS = (DEPTH + 1) // 2
N_CONV_LAYERS = DEPTH // 2
HEAD_DIM = 64
N_HEADS = D_MODEL // HEAD_DIM
N_KV_HEADS = 4
GROUP = N_HEADS // N_KV_HEADS
QKV_DIM = (N_HEADS + 2 * N_KV_HEADS) * HEAD_DIM
WINDOW = 128
BLOCK = 128
ATTN_SCALE = HEAD_DIM ** -0.5
NEG = -1e30
NUM_BUCKETS = 32
MAX_DISTANCE = 128
D_CONV = D_MODEL
CONV_W = 3
N_KEYS = 128
N_EXPERTS = N_KEYS * N_KEYS
PEER_HEADS = 8
D_KEY = 256
D_KEY_HALF = D_KEY // 2
PEER_TOPK = 16
PEER_CHUNK = 256
EPS = 1e-6

kernel_name = "hybrid_swa_shortconv_peer_step"


def rmsnorm(x, g):
    xf = x.astype(jnp.float32)
    y = xf * lax.rsqrt(jnp.mean(xf * xf, axis=-1, keepdims=True) + EPS)
    return (y * g.astype(jnp.float32)).astype(x.dtype)


def t5_bucket(dist):
    n = jnp.maximum(dist, 0)
    max_exact = NUM_BUCKETS // 2
    nf = jnp.maximum(n, 1).astype(jnp.float32)
    large = max_exact + (jnp.log(nf / max_exact) / math.log(MAX_DISTANCE / max_exact)
                         * (NUM_BUCKETS - max_exact)).astype(jnp.int32)
    large = jnp.minimum(large, NUM_BUCKETS - 1)
    return jnp.where(n < max_exact, n, large)


def rel_pos_bias(dist, rel_bias):
    return jnp.transpose(jnp.take(rel_bias, t5_bucket(dist), axis=0), (2, 0, 1))


def qkv_split(xn, w_qkv):
    B, T, _ = xn.shape
    qkv = xn @ w_qkv
    hq = N_HEADS * HEAD_DIM
    hk = N_KV_HEADS * HEAD_DIM
    q = qkv[..., :hq].reshape(B, T, N_HEADS, HEAD_DIM)
    k = qkv[..., hq:hq + hk].reshape(B, T, N_KV_HEADS, HEAD_DIM)
    v = qkv[..., hq + hk:].reshape(B, T, N_KV_HEADS, HEAD_DIM)
    return q, k, v


def window_attention(q, k, v, bias, mask, sinks):
    B, N, Q, _, _ = q.shape
    C = k.shape[2]
    qg = q.reshape(B, N, Q, N_KV_HEADS, GROUP, HEAD_DIM)
    s = jnp.einsum('bnqhgd,bnchd->bnhgqc', qg, k, preferred_element_type=jnp.float32) * ATTN_SCALE
    s = s + bias.reshape(N_KV_HEADS, GROUP, Q, C).astype(jnp.float32)
    s = jnp.where(mask[None, :, None, None], s, NEG)
    sink = sinks.reshape(N_KV_HEADS, GROUP).astype(jnp.float32)[:, :, None, None]
    m = jnp.maximum(jnp.max(s, axis=-1, keepdims=True), sink)
    p = jnp.exp(s - m)
    p = p / (jnp.sum(p, axis=-1, keepdims=True) + jnp.exp(sink - m))
    o = jnp.einsum('bnhgqc,bnchd->bnqhgd', p, v.astype(jnp.float32))
    return o.reshape(B, N, Q, N_HEADS * HEAD_DIM).astype(q.dtype)


def swa_prompt(xn, w_qkv, sinks, w_o, rel_bias):
    B, S, _ = xn.shape
    nb = S // BLOCK
    q, k, v = qkv_split(xn, w_qkv)
    pad = jnp.zeros((B, BLOCK, N_KV_HEADS, HEAD_DIM), k.dtype)
    def band(t):
        prev = jnp.concatenate([pad, t], axis=1)[:, :S].reshape(B, nb, BLOCK, N_KV_HEADS, HEAD_DIM)
        return jnp.concatenate([prev, t.reshape(B, nb, BLOCK, N_KV_HEADS, HEAD_DIM)], axis=2)
    kb, vb = band(k), band(v)
    r = jnp.arange(BLOCK, dtype=jnp.int32)
    c = jnp.arange(2 * BLOCK, dtype=jnp.int32)
    dist = r[:, None] - c[None, :] + BLOCK
    kpos = (jnp.arange(nb, dtype=jnp.int32)[:, None] - 1) * BLOCK + c[None, :]
    mask = ((dist >= 0) & (dist < WINDOW))[None] & (kpos >= 0)[:, None, :]
    o = window_attention(q.reshape(B, nb, BLOCK, N_HEADS, HEAD_DIM), kb, vb,
                         rel_pos_bias(dist, rel_bias), mask, sinks)
    out = o.reshape(B, S, N_HEADS * HEAD_DIM) @ w_o
    rows = min(WINDOW, S)
    return out, k[:, S - rows:], v[:, S - rows:]


def swa_sample(xn, cache_k, cache_v, w_qkv, sinks, w_o, rel_bias):
    B, T, _ = xn.shape
    W = cache_k.shape[1]
    q, k, v = qkv_split(xn, w_qkv)
    kc = jnp.concatenate([cache_k.astype(k.dtype), k], axis=1)
    vc = jnp.concatenate([cache_v.astype(v.dtype), v], axis=1)
    j = jnp.arange(T, dtype=jnp.int32)
    c = jnp.arange(W + T, dtype=jnp.int32)
    dist = j[:, None] + W - c[None, :]
    mask = ((dist >= 0) & (dist < WINDOW))[None]
    o = window_attention(q[:, None], kc[:, None], vc[:, None], rel_pos_bias(dist, rel_bias), mask, sinks)
    out = o.reshape(B, T, N_HEADS * HEAD_DIM) @ w_o
    return out, kc[:, -W:], vc[:, -W:]


def short_conv(xn, buf, w_in, w_conv, w_out):
    T = xn.shape[1]
    bch = xn @ w_in
    b_gate = bch[..., :D_CONV]
    c_gate = bch[..., D_CONV:2 * D_CONV]
    h = bch[..., 2 * D_CONV:]
    u = c_gate * h
    up = jnp.concatenate([buf.astype(u.dtype), u], axis=1)
    y = sum(w_conv[i] * up[:, i:i + T] for i in range(CONV_W))
    return (b_gate * y) @ w_out, up[:, -(CONV_W - 1):]


def peer(xn, w_q, sub_keys, u_tab, v_tab):
    shp = xn.shape
    x2 = xn.reshape(-1, D_MODEL)
    T = x2.shape[0]
    n_chunks = -(-T // PEER_CHUNK)
    xp = jnp.pad(x2, ((0, n_chunks * PEER_CHUNK - T), (0, 0))).reshape(n_chunks, PEER_CHUNK, D_MODEL)

    def one(xc):
        q = (xc @ w_q).reshape(PEER_CHUNK, PEER_HEADS, 2, D_KEY_HALF)
        s = jnp.einsum('thpd,hpkd->thpk', q, sub_keys)
        sv, si = lax.top_k(s, PEER_TOPK)
        cand = (sv[:, :, 0, :, None] + sv[:, :, 1, None, :]).reshape(PEER_CHUNK, PEER_HEADS, -1)
        cidx = (si[:, :, 0, :, None] * N_KEYS + si[:, :, 1, None, :]).reshape(PEER_CHUNK, PEER_HEADS, -1)
        best, pos = lax.top_k(cand, PEER_TOPK)
        idx = jnp.take_along_axis(cidx, pos, axis=-1)
        g = jax.nn.softmax(best.astype(jnp.float32), axis=-1).astype(xc.dtype)
        u = jnp.take(u_tab, idx, axis=0)
        act = jax.nn.gelu(jnp.einsum('td,thkd->thk', xc, u), approximate=False)
        v = jnp.take(v_tab, idx, axis=0)
        return jnp.einsum('thk,thkd->td', g * act, v)

    y = lax.map(one, xp).reshape(-1, D_MODEL)[:T]
    return y.reshape(shp)


def setup_inputs(seed: int = 0) -> dict:
    key = jax.random.key(seed)
    ks = jax.random.split(key, 20)
    n = jax.random.normal
    f32 = jnp.float32
    win_rows = min(WINDOW, PAST_LEN)
    return {
        "x_prompt": n(ks[0], (BATCH, SEQ, D_MODEL), f32),
        "x_sample": n(ks[1], (DEC_BATCH, DEC_SEQ, D_MODEL), f32),
        "cache_k": n(ks[2], (N_ATTN_LAYERS, DEC_BATCH, win_rows, N_KV_HEADS, HEAD_DIM), f32),
        "cache_v": n(ks[3], (N_ATTN_LAYERS, DEC_BATCH, win_rows, N_KV_HEADS, HEAD_DIM), f32),
        "state_conv": n(ks[4], (N_CONV_LAYERS, DEC_BATCH, CONV_W - 1, D_CONV), f32),
        "norm_mix_g": 1.0 + 0.02 * n(ks[5], (DEPTH, D_MODEL), f32),
        "norm_ffn_g": 1.0 + 0.02 * n(ks[6], (DEPTH, D_MODEL), f32),
        "norm_final_g": 1.0 + 0.02 * n(ks[7], (D_MODEL,), f32),
        "rel_bias": 0.5 * n(ks[8], (NUM_BUCKETS, N_HEADS), f32),
        "attn_w_qkv": n(ks[9], (N_ATTN_LAYERS, D_MODEL, QKV_DIM), f32) * D_MODEL ** -0.5,
        "attn_sinks": n(ks[10], (N_ATTN_LAYERS, N_HEADS), f32),
        "attn_w_o": n(ks[11], (N_ATTN_LAYERS, N_HEADS * HEAD_DIM, D_MODEL), f32) * (N_HEADS * HEAD_DIM) ** -0.5,
        "conv_w_in": n(ks[12], (N_CONV_LAYERS, D_MODEL, 3 * D_CONV), f32) * D_MODEL ** -0.5,
        "conv_w": n(ks[13], (N_CONV_LAYERS, CONV_W, D_CONV), f32) * CONV_W ** -0.5,
        "conv_w_out": n(ks[14], (N_CONV_LAYERS, D_CONV, D_MODEL), f32) * D_CONV ** -0.5,
        "peer_w_q": n(ks[15], (DEPTH, D_MODEL, PEER_HEADS * D_KEY), f32) * D_MODEL ** -0.5,
        "peer_sub_keys": n(ks[16], (DEPTH, PEER_HEADS, 2, N_KEYS, D_KEY_HALF), f32) * D_KEY_HALF ** -0.5,
        "peer_u": n(ks[17], (DEPTH, N_EXPERTS, D_MODEL), f32) * D_MODEL ** -0.5,
        "peer_v": n(ks[18], (DEPTH, N_EXPERTS, D_MODEL), f32) * (PEER_HEADS * PEER_TOPK) ** -0.5,
    }


def reference(x_prompt, x_sample, cache_k, cache_v, state_conv, norm_mix_g, norm_ffn_g, norm_final_g,
              rel_bias, attn_w_qkv, attn_sinks, attn_w_o, conv_w_in, conv_w, conv_w_out,
              peer_w_q, peer_sub_keys, peer_u, peer_v):
    hp, hs = x_prompt, x_sample
    kp_l, vp_l, cp_l, ks_l, vs_l, cs_l = [], [], [], [], [], []
    for i in range(DEPTH):
        j = i // N_MIXERS
        g = norm_mix_g[i]
        if i % N_MIXERS == 0:
            mp, kp, vp = swa_prompt(rmsnorm(hp, g), attn_w_qkv[j], attn_sinks[j], attn_w_o[j], rel_bias)
            ms, kn, vn = swa_sample(rmsnorm(hs, g), cache_k[j], cache_v[j], attn_w_qkv[j],
                                    attn_sinks[j], attn_w_o[j], rel_bias)
            kp_l.append(kp); vp_l.append(vp); ks_l.append(kn); vs_l.append(vn)
        else:
            zero_buf = jnp.zeros((hp.shape[0], CONV_W - 1, D_CONV), hp.dtype)
            mp, cp = short_conv(rmsnorm(hp, g), zero_buf, conv_w_in[j], conv_w[j], conv_w_out[j])
            ms, cn = short_conv(rmsnorm(hs, g), state_conv[j], conv_w_in[j], conv_w[j], conv_w_out[j])
            cp_l.append(cp); cs_l.append(cn)
        hp = hp + mp
        hs = hs + ms
        gf = norm_ffn_g[i]
        hp = hp + peer(rmsnorm(hp, gf), peer_w_q[i], peer_sub_keys[i], peer_u[i], peer_v[i])
        hs = hs + peer(rmsnorm(hs, gf), peer_w_q[i], peer_sub_keys[i], peer_u[i], peer_v[i])
    y_prompt = rmsnorm(hp, norm_final_g)
    y_sample = rmsnorm(hs, norm_final_g)
    return (y_prompt, y_sample,
            jnp.stack(kp_l), jnp.stack(vp_l), jnp.stack(cp_l),
            jnp.stack(ks_l), jnp.stack(vs_l), jnp.stack(cs_l))
```

```python
import contextlib
import numpy as np
import concourse.bass as bass
import concourse.mybir as mybir
from concourse.bass_utils import run_bass_kernel_spmd

F32 = mybir.dt.float32
BF16 = mybir.dt.bfloat16
U32 = mybir.dt.uint32
ALU = mybir.AluOpType
AF = mybir.ActivationFunctionType
AX = mybir.AxisListType

ENGS = ['pe', 'act', 'dve', 'pool', 'sp']
NSLOT = 8
NT = 2688
OWN0 = 512
SMP0 = 2560
NEG = -1e30


class Res:
    __slots__ = ('w', 'r')

    def __init__(self):
        self.w = None
        self.r = {}


class FW:
    def __init__(self, nc):
        self.nc = nc
        self.es = contextlib.ExitStack()
        self.q = {e: [] for e in ENGS}
        self.cnt = {e: 0 for e in ENGS}
        self.sems = {}
        for e in ENGS:
            self.sems[e] = self.es.enter_context(nc.semaphore('s_' + e))
        self.seen = {e: {} for e in ENGS}
        self.dslot = {}
        self.dval = {}
        for qn in ('sp', 'act', 'pool'):
            self.dslot[qn] = 0
            for s in range(NSLOT):
                k = ('d', qn, s)
                self.sems[k] = self.es.enter_context(nc.semaphore('d_%s%d' % (qn, s)))
                self.dval[k] = 0
        self.ntens = 0

    def sb(self, shape, dtype):
        self.ntens += 1
        return self.es.enter_context(self.nc.sbuf_tensor('t%d' % self.ntens, list(shape), dtype))

    def ps(self, shape, dtype):
        self.ntens += 1
        return self.es.enter_context(self.nc.psum_tensor('p%d' % self.ntens, list(shape), dtype))

    def _need(self, reads, writes):
        need = {}
        for r in reads:
            if r.w is not None and r.w[1] > need.get(r.w[0], 0):
                need[r.w[0]] = r.w[1]
        for w in writes:
            if w.w is not None and w.w[1] > need.get(w.w[0], 0):
                need[w.w[0]] = w.w[1]
            for k, v in w.r.items():
                if v > need.get(k, 0):
                    need[k] = v
        return need

    def _waits(self, eng, need):
        wl = []
        seen = self.seen[eng]
        for k, v in need.items():
            if eng == 'pe' and k == 'pe':
                continue
            if v > seen.get(k, 0):
                seen[k] = v
                wl.append((k, v))
        return wl

    def op(self, eng, fn, reads=(), writes=(), inc=True):
        wl = self._waits(eng, self._need(reads, writes))
        tgt = self.cnt[eng] + 1
        if inc:
            self.cnt[eng] = tgt
        self.q[eng].append((wl, fn, inc))
        for r in reads:
            if tgt > r.r.get(eng, 0):
                r.r[eng] = tgt
        for w in writes:
            w.w = (eng, tgt)
            w.r = {}

    def dma(self, qn, out, in_, reads=(), writes=(), **kw):
        slot = self.dslot[qn] % NSLOT
        self.dslot[qn] += 1
        key = ('d', qn, slot)
        prev = self.dval[key]
        tgt = prev + 16
        self.dval[key] = tgt
        need = self._need(reads, writes)
        if prev > 0:
            need[key] = max(need.get(key, 0), prev)
        wl = self._waits(qn, need)
        sem = self.sems[key]

        def fn(e, out=out, in_=in_, kw=kw, sem=sem):
            e.dma_start(out=out, in_=in_, **kw).then_inc(sem, 16)
            return None
        self.q[qn].append((wl, fn, False))
        for r in reads:
            r.r[key] = tgt
        for w in writes:
            w.w = (key, tgt)
            w.r = {}

    def barrier(self):
        need = {}
        for k, v in self.dval.items():
            if v > 0:
                need[k] = v
        for e in ENGS:
            if self.cnt[e] > 0:
                need[e] = self.cnt[e]
        for e in ENGS:
            nd = {k: v for k, v in need.items() if k != e}
            wl = self._waits(e, nd)
            if wl:
                self.q[e].append((wl, None, False))

    def emit(self):
        nc = self.nc
        sems = self.sems

        def mk(e):
            def body(engobj):
                mysem = sems[e]
                for wl, fn, inc in self.q[e]:
                    for k, v in wl:
                        engobj.wait_ge(sems[k], v)
                    if fn is None:
                        continue
                    ins = fn(engobj)
                    if inc:
                        ins.then_inc(mysem, 1)
            return body
        with nc.Block() as block:
            block.tensor(mk('pe'))
            block.scalar(mk('act'))
            block.vector(mk('dve'))
            block.gpsimd(mk('pool'))
            block.sync(mk('sp'))


class Arena:
    def __init__(self, t, nbytes):
        self.t = t
        self.n = nbytes
        self.off = 0

    def reset(self):
        self.off = 0

    def get(self, shape, dtype, parts=128):
        shape = list(shape)
        n = int(np.prod(shape))
        es = 2 if dtype == BF16 else 4
        nb = n * es
        o = self.off
        self.off += (nb + 63) // 64 * 64
        assert self.off <= self.n, ("arena overflow", self.off, self.n)
        v = self.t[0:parts, o // 2:(o + nb) // 2]
        if es == 4:
            v = v.bitcast(dtype)
        if len(shape) > 1:
            names = ['d%d' % i for i in range(len(shape))]
            pat = "p (" + " ".join(names) + ") -> p " + " ".join(names)
            v = v.rearrange(pat, **{names[i]: shape[i] for i in range(len(shape))})
        return v


def MM(out, lhsT, rhs, start, stop):
    return lambda e: e.matmul(out, lhsT=lhsT, rhs=rhs, start=start, stop=stop)


def TRN(out, in_, ident):
    return lambda e: e.transpose(out=out, in_=in_, identity=ident)


def ACTF(out, in_, func, **kw):
    return lambda e: e.activation(out=out, in_=in_, func=func, **kw)


def ACP(out, in_):
    return lambda e: e.copy(out=out, in_=in_)


def TT(out, in0, in1, op):
    return lambda e: e.tensor_tensor(out=out, in0=in0, in1=in1, op=op)


def TS(out, in0, s1, s2, op0, op1=None):
    if op1 is None:
        return lambda e: e.tensor_scalar(out=out, in0=in0, scalar1=s1, scalar2=None, op0=op0)
    return lambda e: e.tensor_scalar(out=out, in0=in0, scalar1=s1, scalar2=s2, op0=op0, op1=op1)


def STT(out, in0, scalar, in1, op0, op1):
    return lambda e: e.scalar_tensor_tensor(out=out, in0=in0, scalar=scalar, in1=in1, op0=op0, op1=op1)


def TC(out, in_):
    return lambda e: e.tensor_copy(out=out, in_=in_)


def RCP(out, in_):
    return lambda e: e.reciprocal(out=out, in_=in_)


def RED(out, in_, op):
    return lambda e: e.tensor_reduce(out=out, in_=in_, axis=AX.X, op=op)


def MSET(out, v):
    return lambda e: e.memset(out, v)


def MAX8(out, in_):
    return lambda e: e.max(out=out, in_=in_)


def MIDX(out, in_max, in_values):
    return lambda e: e.max_index(out=out, in_max=in_max, in_values=in_values)


def MREP(out, rep, vals):
    return lambda e: e.match_replace(out=out, in_to_replace=rep, in_values=vals, imm_value=NEG)


def build_program(n_layers=4, do_peer=True, sample_attn=True, dbg=9):
    nc = bass.Bass("TRN2", target_bir_lowering=False)
    fw = FW(nc)

    def din(name, shape, dt=F32):
        return nc.dram_tensor(name, list(shape), dt, kind="ExternalInput").ap()

    def dout(name, shape, dt=F32):
        return nc.dram_tensor(name, list(shape), dt, kind="ExternalOutput").ap()

    xT_d = din("xT", [128, 8, NT])
    flag_d = din("flag", [128, 1])
    ck_d = din("ck", [2, 16, 128, 256])
    cv_d = din("cv", [2, 16, 128, 256])
    stT_d = din("stT", [128, 2, 8, 16, 2])
    gains_d = din("gains", [128, 72])
    relb_d = din("relb", [128, 33 * 16])
    ohb_d = din("ohb", [128, 8, 1056])
    sinks_d = din("sinks", [64, 32])
    wqkv_d = din("wqkv", [2, 128, 8, 1536])
    wo_d = din("wo", [2, 64, 16, 1024])
    win_d = din("win", [2, 128, 24, 1024])
    wout_d = din("wout", [2, 128, 8, 1024])
    convw_d = din("convw", [128, 2, 3, 8])
    NLP = 4 if do_peer else 1
    wq_d = din("wq", [NLP, 16, 128, 1024])
    skt_d = din("skt", [4, 128, 2048])
    ut_d = din("ut", [NLP, 128 if do_peer else 1, 128, 1024])
    vt_d = din("vt", [NLP, 128 if do_peer else 1, 128, 1024])

    ubf = nc.dram_tensor("ubf", [NLP, 128 if do_peer else 1, 128, 1024], BF16, kind="Internal").ap()
    vbf = nc.dram_tensor("vbf", [NLP, 128 if do_peer else 1, 128, 1024], BF16, kind="Internal").ap()
    wqbf = nc.dram_tensor("wqbf", [NLP, 16, 128, 1024], BF16, kind="Internal").ap()
    rcU = [[Res() for _ in range(16)] for _ in range(4)]
    rcV = [[Res() for _ in range(16)] for _ in range(4)]
    rcQ = [[Res() for _ in range(2)] for _ in range(4)]
    cjobs = {}
    for l_ in range(4 if do_peer else 0):
        jl = []
        for b_ in range(2):
            jl.append((wqbf[l_, 8 * b_:8 * b_ + 8], wq_d[l_, 8 * b_:8 * b_ + 8], rcQ[l_][b_]))
        for b_ in range(16):
            jl.append((ubf[l_, 8 * b_:8 * b_ + 8], ut_d[l_, 8 * b_:8 * b_ + 8], rcU[l_][b_]))
            jl.append((vbf[l_, 8 * b_:8 * b_ + 8], vt_d[l_, 8 * b_:8 * b_ + 8], rcV[l_][b_]))
        cjobs[l_] = jl

    def issue_conv(l_, n):
        jl = cjobs.get(l_)
        while jl and n > 0:
            dst, src, r_ = jl.pop(0)
            fw.dma('pool', dst, src, writes=[r_])
            n -= 1

    yp_d = dout("yp", [2048, 1024])
    ys_d = dout("ysm", [64, 1024])
    kvp_d = dout("kvp", [2, 128, 512])
    knw_d = dout("knw", [2, 16, 4, 512])
    cp_d = dout("cp", [2, 2, 1024])
    ksn_d = dout("ksn", [2, 16, 128, 256])
    vsn_d = dout("vsn", [2, 16, 128, 256])
    csn_d = dout("csn", [2, 32, 1024])

    hT = fw.sb([128, 8, NT], F32)
    rH = [Res() for _ in range(NT // 128)]

    def rh(c0, c1):
        return rH[c0 // 128:(c1 + 127) // 128]

    Wreg = fw.sb([128, 32768], BF16)
    rW = Res()
    gains = fw.sb([128, 72], F32); rG = Res()
    convw = fw.sb([128, 2, 3, 8], F32)
    flag = fw.sb([128, 1], F32)
    epsb = fw.sb([128, 1], F32)
    ident_f = fw.sb([128, 128], F32)
    ident_b = fw.sb([128, 128], BF16)
    ones_b = fw.sb([128, 128], BF16)
    iota_b = fw.sb([128, 128], BF16)
    iota16 = fw.sb([128, 16], F32)
    thr16 = fw.sb([128, 16], F32)
    sinkexp = fw.sb([64, 32], F32)
    tbl = fw.sb([128, 3, 16, 128], BF16)
    skt = fw.sb([128, 16, 128], BF16); rSK = Res()
    rC = Res()
    ARENA_BYTES = 42752
    arena_t = fw.sb([128, ARENA_BYTES // 2], BF16)
    A = Arena(arena_t, ARENA_BYTES)
    banks = [fw.ps([128, 512], F32) for _ in range(8)]
    rB = [Res() for _ in range(8)]
    bctr = [0]

    def nb(lo=0, hi=8):
        i = lo + bctr[0] % (hi - lo)
        bctr[0] += 1
        return banks[i], rB[i]

    for kc in range(8):
        fw.dma('sp', hT[:, kc, :], xT_d[:, kc, :], writes=rH)
    fw.dma('sp', gains[:], gains_d[:, :], writes=[rG])
    fw.dma('sp', convw[:], convw_d[:, :, :, :], writes=[rC])
    fw.dma('sp', flag[:], flag_d[:, :], writes=[rC])
    fw.dma('sp', sinkexp[:], sinks_d[:, :], writes=[rC])
    fw.op('pool', MSET(ident_f[:], 0.0), writes=[rC])
    fw.op('pool', lambda e: e.affine_select(out=ident_f[:], in_=ident_f[:], pattern=[[-1, 128]], compare_op=ALU.not_equal,
                                            fill=1.0, base=0, channel_multiplier=1), reads=[rC], writes=[rC])
    fw.op('pool', TC(ident_b[:], ident_f[:]), reads=[rC], writes=[rC])
    fw.op('pool', MSET(ones_b[:], 1.0), writes=[rC])
    fw.op('pool', MSET(epsb[:], 1e-6), writes=[rC])
    fw.op('pool', lambda e: e.iota(iota_b[:], pattern=[[1, 128]], base=0, channel_multiplier=0,
                                   allow_small_or_imprecise_dtypes=True), writes=[rC])
    fw.op('pool', lambda e: e.iota(iota16[:], pattern=[[1, 16]], base=0, channel_multiplier=0,
                                   allow_small_or_imprecise_dtypes=True), writes=[rC])
    fw.op('pool', lambda e: e.iota(thr16[:], pattern=[[16, 16]], base=16, channel_multiplier=0,
                                   allow_small_or_imprecise_dtypes=True), writes=[rC])
    fw.op('act', ACTF(sinkexp[:], sinkexp[:], AF.Exp), reads=[rC], writes=[rC])
    def build_tbl():
        A.reset()
        ohb = A.get([33, 2, 128], BF16)
        relb = A.get([33 * 16], F32)
        acc = A.get([2, 16, 128], F32)
        rS = Res()
        fw.dma('pool', ohb.rearrange("p b w q -> p (b w q)").rearrange("p (a x) -> p a x", a=8), ohb_d[:, :, :], writes=[rS])
        fw.dma('sp', relb, relb_d[:, :], writes=[rS])
        fw.op('dve', TS(relb[:, 0:512], relb[:, 0:512], 8.0, None, ALU.mult), reads=[rS], writes=[rS])
        for h in range(16):
            for b in range(33):
                sc1 = relb[:, b * 16 + h:b * 16 + h + 1]
                if b == 0:
                    fw.op('dve', TS(acc[:, :, h, :], ohb[:, b, :, :], sc1, None, ALU.mult), reads=[rS], writes=[rS])
                else:
                    fw.op('dve', STT(acc[:, :, h, :], ohb[:, b, :, :], sc1, acc[:, :, h, :], ALU.mult, ALU.add), reads=[rS], writes=[rS])
        fw.op('dve', TC(tbl[:, 0:2, :, :], acc), reads=[rS], writes=[rC])
        fw.op('dve', TS(tbl[:, 2, :, :], acc[:, 0, :, :], flag[:, 0:1], None, ALU.add), reads=[rS, rC], writes=[rC])
        fw.barrier()

    fw.barrier()

    def norm(c0, C, gi, xn, rxn, sq, rsq, rs, rrs, blo=0, bhi=8):
        hr = rh(c0, c0 + C)
        fw.op('act', ACTF(sq[:, :, 0:C], hT[:, :, c0:c0 + C], AF.Square), reads=hr, writes=[rsq])
        bk, rb = nb(blo, bhi)
        for kc in range(8):
            fw.op('pe', MM(bk[:, 0:C], ones_b[:], sq[:, kc, 0:C], kc == 0, kc == 7), reads=[rsq, rC], writes=[rb], inc=(kc == 7))
        fw.op('act', ACTF(rs[:, 0:C], bk[:, 0:C], AF.Sqrt, scale=1.0 / 1024.0, bias=epsb[:, 0:1]), reads=[rb, rC], writes=[rrs])
        fw.op('dve', RCP(rs[:, 0:C], rs[:, 0:C]), reads=[rrs], writes=[rrs])
        for kc in range(8):
            fw.op('dve', STT(xn[:, kc, 0:C], hT[:, kc, c0:c0 + C], gains[:, gi * 8 + kc:gi * 8 + kc + 1], rs[:, 0:C], ALU.mult, ALU.mult),
                  reads=hr + [rrs, rG], writes=[rxn])

    def proj(wfn, M, nch, xn, rxn, C, evac, wres):
        for m in range(nch):
            bk, rb = nb()
            for kc in range(8):
                fw.op('pe', MM(bk[0:M, 0:C], wfn(kc, m), xn[:, kc, 0:C], kc == 0, kc == 7), reads=[wres, rxn], writes=[rb], inc=(kc == 7))
            evac(m, bk, rb)

    def resid_add(c0, C, m, src, rb):
        fw.op('dve', TT(hT[:, m, c0:c0 + C], src, hT[:, m, c0:c0 + C], ALU.add), reads=[rb] + rh(c0, c0 + C), writes=rh(c0, c0 + C))

    def hq(ap, h=4):
        return ap.rearrange("p (h q) -> p h q", h=h)

    def attn_layer(l, j, c_kv):
        build_tbl()
        A.reset()
        wqkv = Wreg[:, 0:12288].rearrange("p (k n) -> p k n", k=8)
        wo = Wreg[0:64, 12288:12288 + 16384].rearrange("p (h n) -> p h n", h=16)
        fw.dma('pool', wqkv, wqkv_d[j], writes=[rW])
        fw.dma('pool', wo, wo_d[j], writes=[rW])
        if l == 0:
            issue_conv(0, 1000)
        xn = A.get([8, 128], BF16); rxn = Res()
        sq = A.get([8, 128], BF16); rsq = Res()
        rs = A.get([128], F32); rrs = Res()
        QT = A.get([16, 128], BF16, parts=64); rQ = Res()
        KT = A.get([4, 256], BF16, parts=64); rK = Res()
        V3 = A.get([2, 256], BF16); rV = Res()
        AT = A.get([16, 128], BF16, parts=64); rA = Res()
        PT = [A.get([4, 128], BF16) for _ in range(4)]; rP = [Res() for _ in range(4)]
        tmp = A.get([512], F32, parts=64); rT = Res()
        kvf = A.get([512], F32); rKV = Res()
        kst = A.get([4, 256], F32); rKS = Res()
        KTc = A.get([4, 4, 128], BF16, parts=64); rKC = Res()
        Vc = A.get([4, 256], BF16); rVC = Res()
        Vn = A.get([4, 256], BF16, parts=4); rVN = Res()
        knew = [A.get([512], F32, parts=4) for _ in range(2)]; rKN = [Res() for _ in range(2)]
        print("attn arena", A.off)
        pctr = [0]

        def qkv_chunk(c0, C):
            norm(c0, C, l, xn, rxn, sq, rsq, rs, rrs)

            def ev_q(m, bk, rb):
                fw.op('act', ACP(QT[:, m, 0:C], bk[0:64, 0:C]), reads=[rb], writes=[rQ])
            proj(lambda kc, m: wqkv[:, kc, m * 64:(m + 1) * 64], 64, 16, xn, rxn, C, ev_q, rW)

            def ev_k(m, bk, rb):
                fw.op('act', ACP(KT[:, m, 128:128 + C], bk[0:64, 0:C]), reads=[rb], writes=[rK])
            if dbg >= 0.25:
                proj(lambda kc, m: wqkv[:, kc, 1024 + m * 64:1024 + (m + 1) * 64], 64, 4, xn, rxn, C, ev_k, rW)

        def kv_tok(blk, is_last_own):
            bk, rb = nb()
            for kc in range(8):
                fw.op('pe', MM(bk[:, :], xn[:, kc, blk * 128:(blk + 1) * 128], wqkv[:, kc, 1024:1536], kc == 0, kc == 7),
                      reads=[rxn, rW], writes=[rb], inc=(kc == 7))
            fw.op('act', ACP(V3[:, 1 + blk, :], bk[:, 256:512]), reads=[rb], writes=[rV])
            if is_last_own:
                fw.op('act', ACP(kvf, bk[:, :]), reads=[rb], writes=[rKV])
                fw.dma('sp', kvp_d[j], kvf, reads=[rKV])

        def attn_block(blk, first):
            wsel = 2 if first else 0
            for g in range(4):
                sp_, rsp = nb()
                sc_, rsc = nb()
                qv = QT[:, 4 * g:4 * g + 4, blk * 128:(blk + 1) * 128]
                fw.op('pe', MM(hq(sp_[:, :]), KT[:, g, blk * 128:(blk + 1) * 128], qv, True, False), reads=[rK, rQ], writes=[rsp], inc=False)
                fw.op('pe', MM(hq(sp_[:, :]), ident_b[:], tbl[:, wsel, 4 * g:4 * g + 4, :], False, True), reads=[rC], writes=[rsp])
                fw.op('pe', MM(hq(sc_[:, :]), KT[:, g, (blk + 1) * 128:(blk + 2) * 128], qv, True, False), reads=[rK, rQ], writes=[rsc], inc=False)
                fw.op('pe', MM(hq(sc_[:, :]), ident_b[:], tbl[:, 1, 4 * g:4 * g + 4, :], False, True), reads=[rC], writes=[rsc])
                ia = pctr[0] % 2
                pctr[0] += 1
                Pp, rPp = PT[2 * ia], rP[2 * ia]
                Pc, rPc = PT[2 * ia + 1], rP[2 * ia + 1]
                fw.op('act', ACTF(Pp, hq(sp_[:, :]), AF.Exp, scale=0.125), reads=[rsp], writes=[rPp])
                fw.op('act', ACTF(Pc, hq(sc_[:, :]), AF.Exp, scale=0.125), reads=[rsc], writes=[rPc])
                ob, rob = nb()
                db, rdb = nb()
                fw.op('pe', MM(hq(ob[0:64, :]), V3[:, blk, g * 64:(g + 1) * 64], Pp, True, False), reads=[rV, rPp], writes=[rob], inc=False)
                fw.op('pe', MM(hq(ob[0:64, :]), V3[:, blk + 1, g * 64:(g + 1) * 64], Pc, False, True), reads=[rV, rPc], writes=[rob])
                fw.op('pe', MM(hq(db[0:64, :]), ones_b[:, 0:64], Pp, True, False), reads=[rPp, rC], writes=[rdb], inc=False)
                fw.op('pe', MM(hq(db[0:64, :]), ones_b[:, 0:64], Pc, False, True), reads=[rPc, rC], writes=[rdb])
                sk = sinkexp[:, j * 16 + 4 * g:j * 16 + 4 * g + 4].unsqueeze(2).to_broadcast([64, 4, 128])
                tv = hq(tmp)
                fw.op('dve', TT(tv, hq(db[0:64, :]), sk, ALU.add), reads=[rdb, rC], writes=[rT])
                fw.op('dve', RCP(tmp, tmp), reads=[rT], writes=[rT])
                fw.op('dve', TT(AT[:, 4 * g:4 * g + 4, blk * 128:(blk + 1) * 128], hq(ob[0:64, :]), tv, ALU.mult), reads=[rob, rT], writes=[rA])

        def out_proj(c0, a0, C):
            for m in range(8):
                bk, rb = nb()
                for hh in range(16):
                    fw.op('pe', MM(bk[:, 0:C - a0], wo[:, hh, m * 128:(m + 1) * 128], AT[:, hh, a0:C], hh == 0, hh == 15),
                          reads=[rW, rA], writes=[rb], inc=(hh == 15))
                resid_add(c0 + a0, C - a0, m, bk[:, 0:C - a0], rb)

        def roll():
            fw.op('dve', TC(KT[:, :, 0:128], KT[:, :, 128:256]), reads=[rK], writes=[rK])
            fw.op('dve', TC(V3[:, 0, :], V3[:, 1, :]), reads=[rV], writes=[rV])

        c0 = c_kv
        firstchunk = True
        while c0 < SMP0:
            if dbg < 0.15:
                break
            qkv_chunk(c0, 128)
            if dbg >= 0.3:
                kv_tok(0, is_last_own=(c0 == SMP0 - 128) and dbg >= 0.5)
            if not firstchunk and dbg >= 2:
                bidx = c0 // 128 - 4
                attn_block(0, first=(bidx <= 0))
                if dbg >= 3:
                    out_proj(c0, 0, 128)
            if dbg >= 0.4:
                roll()
            firstchunk = False
            c0 += 128
        if not sample_attn:
            fw.barrier()
            return
        c0 = SMP0
        qkv_chunk(c0, 128)
        fw.op('dve', MSET(AT[:, :, 0:128], 0.0), writes=[rA])
        Ppv = PT[0].rearrange("p a b -> p (a b)")
        Pcv = PT[1].rearrange("p a b -> p (a b)")
        rPp, rPc = rP[0], rP[1]
        for s4 in range(4):
            fw.dma('sp', kst, cv_d[j, 4 * s4:4 * s4 + 4].rearrange("s c n -> c s n"), writes=[rKS])
            fw.op('act', ACP(Vc, kst), reads=[rKS], writes=[rVC])
            fw.dma('sp', vsn_d[j, 4 * s4:4 * s4 + 4].rearrange("s c n -> c s n"), kst, reads=[rKS])
            fw.dma('sp', kst, ck_d[j, 4 * s4:4 * s4 + 4].rearrange("s c n -> c s n"), writes=[rKS])
            fw.dma('sp', ksn_d[j, 4 * s4:4 * s4 + 4].rearrange("s c n -> c s n"), kst, reads=[rKS])
            for si in range(4):
                s = 4 * s4 + si
                bk, rb = nb()
                for kc in range(8):
                    fw.op('pe', MM(bk[0:4, :], xn[:, kc, 4 * s:4 * s + 4], wqkv[:, kc, 1024:1536], kc == 0, kc == 7),
                          reads=[rxn, rW], writes=[rb], inc=(kc == 7))
                fw.op('act', ACP(Vn[:, si, :], bk[0:4, 256:512]), reads=[rb], writes=[rVN])
                kn, rkn = knew[s % 2], rKN[s % 2]
                fw.op('act', ACP(kn, bk[0:4, :]), reads=[rb], writes=[rkn])
                fw.dma('sp', knw_d[j, s], kn, reads=[rkn])
                bk, rb = nb()
                for g in range(4):
                    fw.op('pe', TRN(bk[0:64, g * 128:(g + 1) * 128], kst[:, si, g * 64:(g + 1) * 64], ident_f[:]),
                          reads=[rKS, rC], writes=[rb], inc=(g == 3))
                fw.op('act', ACP(KTc[:, si, :, :], bk[0:64, :].rearrange("p (g c) -> p g c", g=4)), reads=[rb], writes=[rKC])
            sp_, rsp = nb()
            sc_, rsc = nb()
            for si in range(4):
                s = 4 * s4 + si
                for g in range(4):
                    qv = QT[:, 4 * g:4 * g + 4, 4 * s:4 * s + 4]
                    o0 = si * 64 + g * 16
                    last = (si == 3 and g == 3)
                    fw.op('pe', MM(hq(sp_[:, o0:o0 + 16]), KTc[:, si, g, :], qv, True, False), reads=[rKC, rQ], writes=[rsp], inc=False)
                    fw.op('pe', MM(hq(sp_[:, o0:o0 + 16]), ident_b[:], tbl[:, 0, 4 * g:4 * g + 4, 0:4], False, True), reads=[rC], writes=[rsp], inc=False)
                    fw.op('pe', MM(hq(sc_[0:4, o0:o0 + 16]), KT[:, g, 128 + 4 * s:128 + 4 * s + 4], qv, True, False), reads=[rK, rQ], writes=[rsc], inc=False)
                    fw.op('pe', MM(hq(sc_[0:4, o0:o0 + 16]), ident_b[0:4, 0:4], tbl[0:4, 1, 4 * g:4 * g + 4, 0:4], False, True),
                          reads=[rC], writes=[rsp, rsc], inc=last)
            fw.op('act', ACTF(Ppv[:, 0:256], sp_[:, 0:256], AF.Exp, scale=0.125), reads=[rsp], writes=[rPp])
            fw.op('act', ACTF(Pcv[0:4, 0:256], sc_[0:4, 0:256], AF.Exp, scale=0.125), reads=[rsc], writes=[rPc])
            ob, rob = nb()
            db, rdb = nb()
            for si in range(4):
                for g in range(4):
                    o0 = si * 64 + g * 16
                    fw.op('pe', MM(ob[0:64, o0:o0 + 16], Vc[:, si, g * 64:(g + 1) * 64], Ppv[:, o0:o0 + 16], True, False), reads=[rVC, rPp], writes=[rob], inc=False)
                    fw.op('pe', MM(ob[0:64, o0:o0 + 16], Vn[:, si, g * 64:(g + 1) * 64], Pcv[0:4, o0:o0 + 16], False, True), reads=[rVN, rPc], writes=[rob], inc=False)
                fw.op('pe', MM(db[0:64, si * 64:(si + 1) * 64], ones_b[:, 0:64], Ppv[:, si * 64:(si + 1) * 64], True, False), reads=[rPp, rC], writes=[rdb], inc=False)
                fw.op('pe', MM(db[0:64, si * 64:(si + 1) * 64], ones_b[0:4, 0:64], Pcv[0:4, si * 64:(si + 1) * 64], False, True),
                      reads=[rPc, rC], writes=[rdb, rob], inc=(si == 3))
            sk = sinkexp[:, j * 16:j * 16 + 16].unsqueeze(1).unsqueeze(3).to_broadcast([64, 4, 16, 4])
            tv = tmp[:, 0:256].rearrange("p (s h q) -> p s h q", s=4, h=16)
            fw.op('dve', TT(tv, db[0:64, 0:256].rearrange("p (s h q) -> p s h q", s=4, h=16), sk, ALU.add), reads=[rdb, rC], writes=[rT])
            fw.op('dve', RCP(tmp[:, 0:256], tmp[:, 0:256]), reads=[rT], writes=[rT])
            fw.op('dve', TT(AT[:, :, 16 * s4:16 * s4 + 16].rearrange("p h (s q) -> p s h q", s=4),
                            ob[0:64, 0:256].rearrange("p (s h q) -> p s h q", s=4, h=16), tv, ALU.mult), reads=[rob, rT], writes=[rA])
        out_proj(c0, 0, 128)
        fw.barrier()

    def conv_layer(l, j, c_mix):
        A.reset()
        win = Wreg[:, 0:24576].rearrange("p (k n) -> p k n", k=8)
        wout = Wreg[:, 24576:32768].rearrange("p (k n) -> p k n", k=8)
        fw.dma('pool', Wreg[:, 0:24576].rearrange("p (a b) -> p a b", a=24), win_d[j], writes=[rW])
        fw.dma('pool', wout, wout_d[j], writes=[rW])
        xn = A.get([8, 256], BF16); rxn = Res()
        sq = A.get([8, 256], BF16); rsq = Res()
        rs = A.get([256], F32); rrs = Res()
        bT = A.get([8, 256], BF16); rb_ = Res()
        cT = A.get([8, 256], F32); rc_ = Res()
        uT = A.get([8, 258], F32); ru = Res()
        uS = A.get([8, 16, 6], F32); ruS = Res()
        yt = A.get([256], F32); ry = Res()
        gT = A.get([8, 256], BF16); rg = Res()
        ot = A.get([1024], F32, parts=32); rot = Res()
        print("conv arena", A.off)
        fw.op('pool', MSET(uT[:, :, 0:2], 0.0), writes=[ru])
        fw.op('pool', MSET(gT, 0.0), writes=[rg])
        fw.dma('sp', uS[:, :, :, 0:2], stT_d[:, j, :, :, :], writes=[ruS])

        def s16(ap):
            return ap.rearrange("p (s q) -> p s q", s=16)

        def chunk(c0, C, sample):
            norm(c0, C, l, xn, rxn, sq, rsq, rs, rrs)

            def ev(m, bk, rb):
                if m < 8:
                    fw.op('act', ACP(bT[:, m, 0:C], bk[:, 0:C]), reads=[rb], writes=[rb_])
                elif m < 16:
                    fw.op('act', ACP(cT[:, m - 8, 0:C], bk[:, 0:C]), reads=[rb], writes=[rc_])
                else:
                    kc = m - 16
                    if not sample:
                        fw.op('dve', TT(uT[:, kc, 2:2 + C], bk[:, 0:C], cT[:, kc, 0:C], ALU.mult), reads=[rb, rc_], writes=[ru])
                    else:
                        fw.op('dve', TT(uS[:, kc, :, 2:6], s16(bk[:, 0:64]), s16(cT[:, kc, 0:64]), ALU.mult), reads=[rb, rc_], writes=[ruS])
            proj(lambda kc, m: win[:, kc, m * 128:(m + 1) * 128], 128, 24, xn, rxn, C, ev, rW)
            for kc in range(8):
                if not sample:
                    u0, u1, u2 = uT[:, kc, 0:C], uT[:, kc, 1:1 + C], uT[:, kc, 2:2 + C]
                    yv, bv, gv = yt[:, 0:C], bT[:, kc, 0:C], gT[:, kc, 0:C]
                    rr = ru
                else:
                    u0, u1, u2 = uS[:, kc, :, 0:4], uS[:, kc, :, 1:5], uS[:, kc, :, 2:6]
                    yv, bv, gv = s16(yt[:, 0:64]), s16(bT[:, kc, 0:64]), s16(gT[:, kc, 0:64])
                    rr = ruS
                w0 = convw[:, j, 0, kc:kc + 1]
                w1 = convw[:, j, 1, kc:kc + 1]
                w2 = convw[:, j, 2, kc:kc + 1]
                fw.op('dve', TS(yv, u2, w2, None, ALU.mult), reads=[rr, rC], writes=[ry])
                fw.op('dve', STT(yv, u1, w1, yv, ALU.mult, ALU.add), reads=[rr, ry, rC], writes=[ry])
                fw.op('dve', STT(yv, u0, w0, yv, ALU.mult, ALU.add), reads=[rr, ry, rC], writes=[ry])
                fw.op('dve', TT(gv, yv, bv, ALU.mult), reads=[ry, rb_], writes=[rg])

            def ev_o(m, bk, rb):
                resid_add(c0, C, m, bk[:, 0:C], rb)
            proj(lambda kc, m: wout[:, kc, m * 128:(m + 1) * 128], 128, 8, gT, rg, C, ev_o, rW)

        c0 = c_mix
        while c0 < SMP0:
            C = min(256, SMP0 - c0)
            chunk(c0, C, False)
            if c0 + C == SMP0:
                for half in range(2):
                    bk, rb = nb()
                    for k4 in range(4):
                        fw.op('pe', TRN(bk[0:2, k4 * 128:(k4 + 1) * 128], uT[:, half * 4 + k4, C:C + 2], ident_f[:]), reads=[ru, rC], writes=[rb], inc=(k4 == 3))
                    fw.op('dve', TC(ot[0:2, half * 512:(half + 1) * 512], bk[0:2, :]), reads=[rb], writes=[rot])
                fw.dma('sp', cp_d[j], ot[0:2, :], reads=[rot])
            else:
                fw.op('dve', TC(uT[:, :, 0:2], uT[:, :, C:C + 2]), reads=[ru], writes=[ru])
            c0 += C
        chunk(SMP0, 128, True)
        for half in range(2):
            bk, rb = nb()
            for k4 in range(4):
                ytv = yt[:, 0:32].rearrange("p (s r) -> p s r", s=16)
                fw.op('dve', TC(ytv, uS[:, half * 4 + k4, :, 4:6]), reads=[ruS], writes=[ry])
                fw.op('pe', TRN(bk[0:32, k4 * 128:(k4 + 1) * 128], yt[:, 0:32], ident_f[:]), reads=[ry, rC], writes=[rb], inc=True)
            fw.op('dve', TC(ot[0:32, half * 512:(half + 1) * 512], bk[0:32, :]), reads=[rb], writes=[rot])
        fw.dma('sp', csn_d[j], ot[0:32, :], reads=[rot])
        fw.barrier()

    def peer_layer(l, c_peer):
        A.reset()
        Wv = Wreg[:, :].rearrange("p (i t) -> p i t", i=128)
        ysb = Wreg[:, 0:4096].bitcast(F32).rearrange("p (t d) -> p t d", t=2)
        fw.dma('pool', skt[:].rearrange("p c k -> p (c k)"), skt_d[l], writes=[rSK])
        tflat = tbl[:].rearrange("p a b c -> p (a b c)")
        uring = [A.get([8, 128], BF16) for _ in range(2)] + [tflat[:, k * 1024:(k + 1) * 1024].rearrange("p (k j) -> p k j", k=8) for k in range(3)]
        vring = [A.get([1024], BF16) for _ in range(2)] + [tflat[:, k * 1024:(k + 1) * 1024] for k in range(3, 6)]
        NR = 5
        rUR = [Res() for _ in range(NR)]
        rVR = [Res() for _ in range(NR)]
        xn = A.get([8, 256], BF16); rxn = Res()
        xnq = A.get([8, 128], BF16); rxq = Res()
        rs = A.get([256], F32); rrs = Res()
        qT = A.get([16, 128], BF16); rq = Res()
        bigraw = A.get([4096], BF16); rbig = Res()
        big = bigraw.bitcast(F32)
        sq = bigraw[:, 0:2048].rearrange("p (k c) -> p k c", k=8); rsq = rbig
        sv = A.get([16, 16], F32); rsv = Res()
        si = A.get([16, 16], U32); rsi = Res()
        sif = A.get([16, 16], F32); rsif = Res()
        tm = A.get([256], F32); rtm = Res()
        best = A.get([8, 16], F32); rbest = Res()
        pos = A.get([8, 16], U32); rpos = Res()
        posf = A.get([8, 16], F32); paf = A.get([8, 16], F32); pbf = A.get([8, 16], F32); rpp = Res()
        sel = A.get([3, 8, 16], F32); rsel = Res()
        gs = A.get([8], F32); rgs = Res()
        trio = A.get([3, 256], BF16); rtrio = Res()
        NO = 4
        O2 = [A.get([128], BF16) for _ in range(NO)]; rO2 = [Res() for _ in range(NO)]
        O1 = [A.get([128], BF16) for _ in range(NO)]; rO1 = [Res() for _ in range(NO)]
        asb = [A.get([256], BF16) for _ in range(2)]; rasb = [Res() for _ in range(2)]
        wa = [A.get([256], BF16) for _ in range(2)]; rwa = [Res() for _ in range(2)]
        rAB = [rB[4], rB[7]]
        print("peer arena", A.off)
        uctr = [0]
        vctr = [0]
        octr = [0]

        def stream_u(src, rsrc):
            s = uctr[0] % NR
            uctr[0] += 1
            fw.dma('sp', uring[s].rearrange("p k j -> p (k j)"), src, reads=[rsrc], writes=[rUR[s]])
            return uring[s], rUR[s]

        def stream_v(src, rsrc):
            s = vctr[0] % NR
            vctr[0] += 1
            fw.dma('sp', vring[s], src, reads=[rsrc], writes=[rVR[s]])
            return vring[s], rVR[s]

        def route_gen(c0, G):
            for t0 in range(0, G, 128):
                norm(c0 + t0, 128, 4 + l, xnq, rxq, sq, rsq, rs, rrs, 5, 7)
                pre = [None] * 17
                pre[0] = stream_u(wqbf[l, 0], rcQ[l][0])
                yield
                for m in range(16):
                    if m + 1 < 16:
                        pre[m + 1] = stream_u(wqbf[l, m + 1], rcQ[l][(m + 1) // 8])
                    ub, rub = pre[m]
                    bk, rb = nb(5, 7)
                    for kc in range(8):
                        fw.op('pe', MM(bk[:, 0:128], ub[:, kc, :], xnq[:, kc, :], kc == 0, kc == 7), reads=[rub, rxq], writes=[rb], inc=(kc == 7))
                    fw.op('act', ACP(qT[:, m, :], bk[:, 0:128]), reads=[rb], writes=[rq])
                    yield
                yield
                for hf in range(2):
                    bks = [nb(5, 7) for _ in range(2)]
                    for cc in range(8):
                        c = hf * 8 + cc
                        bk, rb = bks[cc // 4]
                        fw.op('pe', MM(bk[:, (cc % 4) * 128:(cc % 4 + 1) * 128], qT[:, c, :], skt[:, c, :], True, True),
                              reads=[rq, rSK], writes=[rb], inc=(cc % 4 == 3))
                    for k2 in range(2):
                        bk, rb = bks[k2]
                        o = hf * 1024 + k2 * 512
                        fw.op('act', ACP(big[:, o:o + 512], bk[:, :]), reads=[rb], writes=[rbig])
                    yield
                for c in range(16):
                    scc = big[:, c * 128:(c + 1) * 128]
                    fw.op('dve', MAX8(sv[:, c, 0:8], scc), reads=[rbig], writes=[rsv])
                    fw.op('dve', MIDX(si[:, c, 0:8], sv[:, c, 0:8], scc), reads=[rbig, rsv], writes=[rsi])
                    fw.op('dve', MREP(tm[:, 0:128], sv[:, c, 0:8], scc), reads=[rbig, rsv], writes=[rtm])
                    fw.op('dve', MAX8(sv[:, c, 8:16], tm[:, 0:128]), reads=[rtm], writes=[rsv])
                    fw.op('dve', MIDX(si[:, c, 8:16], sv[:, c, 8:16], tm[:, 0:128]), reads=[rtm, rsv], writes=[rsi])
                    yield
                fw.op('dve', TC(sif, si), reads=[rsi], writes=[rsif])
                svv = sv.rearrange("p (h two) a -> p h two a", two=2)
                sfv = sif.rearrange("p (h two) a -> p h two a", two=2)
                cand = big.rearrange("p (h a b) -> p h a b", h=8, a=16)
                cand2 = big.rearrange("p (h x) -> p h x", h=8)
                fw.op('dve', TT(cand, svv[:, :, 0, :].unsqueeze(3).to_broadcast([128, 8, 16, 16]),
                                svv[:, :, 1, :].unsqueeze(2).to_broadcast([128, 8, 16, 16]), ALU.add), reads=[rsv], writes=[rbig])
                yield
                for h in range(8):
                    fw.op('dve', MAX8(best[:, h, 0:8], cand2[:, h, :]), reads=[rbig], writes=[rbest])
                    fw.op('dve', MIDX(pos[:, h, 0:8], best[:, h, 0:8], cand2[:, h, :]), reads=[rbig, rbest], writes=[rpos])
                    fw.op('dve', MREP(tm, best[:, h, 0:8], cand2[:, h, :]), reads=[rbig, rbest], writes=[rtm])
                    fw.op('dve', MAX8(best[:, h, 8:16], tm), reads=[rtm], writes=[rbest])
                    fw.op('dve', MIDX(pos[:, h, 8:16], best[:, h, 8:16], tm), reads=[rtm, rbest], writes=[rpos])
                    yield
                fw.op('dve', TC(posf, pos), reads=[rpos], writes=[rpp])
                ge = big[:, 0:1920].rearrange("p (x m) -> p x m", m=15)
                pfl = posf.rearrange("p h k -> p (h k)")
                fw.op('dve', TT(ge, pfl.unsqueeze(2).to_broadcast([128, 128, 15]), thr16[:, 0:15].unsqueeze(1).to_broadcast([128, 128, 15]), ALU.is_ge),
                      reads=[rpp, rC], writes=[rbig])
                fw.op('dve', RED(paf.rearrange("p h k -> p (h k)"), ge, ALU.add), reads=[rbig], writes=[rpp])
                fw.op('dve', STT(pbf, paf, -16.0, posf, ALU.mult, ALU.add), reads=[rpp], writes=[rpp])
                yield
                eq = big.rearrange("p (h k a) -> p h k a", h=8, k=16)
                io = iota16[:, :].unsqueeze(1).unsqueeze(1).to_broadcast([128, 8, 16, 16])
                for which, pf in ((0, paf), (1, pbf)):
                    fw.op('dve', TT(eq, pf.unsqueeze(3).to_broadcast([128, 8, 16, 16]), io, ALU.is_equal), reads=[rpp, rC], writes=[rbig])
                    fw.op('dve', TT(eq, eq, sfv[:, :, which, :].unsqueeze(2).to_broadcast([128, 8, 16, 16]), ALU.mult), reads=[rbig, rsif], writes=[rbig])
                    fw.op('dve', RED(sel[:, which, :, :], eq, ALU.add), reads=[rbig], writes=[rsel])
                    yield
                gx = sel[:, 2, :, :]
                fw.op('dve', TT(gx, best, best[:, :, 0:1].to_broadcast([128, 8, 16]), ALU.subtract), reads=[rbest], writes=[rsel])
                fw.op('act', ACTF(gx, gx, AF.Exp), reads=[rsel], writes=[rsel])
                fw.op('dve', RED(gs, gx, ALU.add), reads=[rsel], writes=[rgs])
                fw.op('dve', RCP(gs, gs), reads=[rgs], writes=[rgs])
                fw.op('dve', TT(gx, gx, gs.unsqueeze(2).to_broadcast([128, 8, 16]), ALU.mult), reads=[rsel, rgs], writes=[rsel])
                yield
                yield
                yield
                bk, rb = nb(5, 7)
                for w3 in range(3):
                    fw.op('pe', TRN(bk[:, w3 * 128:(w3 + 1) * 128], sel[:, w3, :, :].rearrange("p h k -> p (h k)"), ident_f[:]),
                          reads=[rsel, rC], writes=[rb], inc=(w3 == 2))
                fw.op('act', ACP(trio[:, :, t0:t0 + 128], bk[:, 0:384].rearrange("p (w t) -> p w t", w=3)), reads=[rb], writes=[rtrio])
                yield

        def expert_group(c0, G, nxt):
            for t4 in range(0, G, 4):
                bk, rb = nb(5, 7)
                for tt in range(4):
                    t = t4 + tt
                    s = octr[0] % NO
                    octr[0] += 1
                    fw.op('dve', TS(O2[s], iota_b[:], trio[:, 1, t:t + 1], None, ALU.is_equal), reads=[rtrio, rC], writes=[rO2[s]])
                    fw.op('dve', TS(O1[s], iota_b[:], trio[:, 0, t:t + 1], trio[:, 2, t:t + 1], ALU.is_equal, ALU.mult), reads=[rtrio, rC], writes=[rO1[s]])
                    fw.op('pe', MM(bk[:, tt * 128:(tt + 1) * 128], O2[s], O1[s], True, True), reads=[rO2[s], rO1[s]], writes=[rb], inc=True)
                fw.op('act', ACP(Wv[:, :, t4:t4 + 4], bk[:, :].rearrange("p (t i) -> p i t", t=4)), reads=[rb], writes=[rW])
            norm(c0, G, 4 + l, xn, rxn, sq, rsq, rs, rrs, 5, 7)
            NTT = G // 128
            ybanks = [(banks[q], rB[q]) for q in range(2 * NTT)]

            def emit_y(i, a2, vb, rvb):
                for tt in range(NTT):
                    for dh in range(2):
                        yb, ryb = ybanks[tt * 2 + dh]
                        fw.op('pe', MM(yb[:, :], wa[a2][:, tt * 128:(tt + 1) * 128], vb[:, dh * 512:(dh + 1) * 512], i == 0, i == 127),
                              reads=[rvb, rwa[a2]], writes=[ryb], inc=(tt == NTT - 1 and dh == 1))

            prev = None
            for i in range(128):
                ub, rub = stream_u(ubf[l, i], rcU[l][i // 8])
                vb, rvb = stream_v(vbf[l, i], rcV[l][i // 8])
                a2 = i % 2
                ab = banks[4] if a2 == 0 else banks[7]
                for kc in range(8):
                    fw.op('pe', MM(ab[:, 0:G], ub[:, kc, :], xn[:, kc, 0:G], kc == 0, kc == 7), reads=[rub, rxn], writes=[rAB[a2]], inc=(kc == 7))
                fw.op('act', ACTF(asb[a2][:, 0:G], ab[:, 0:G], AF.Gelu), reads=[rAB[a2]], writes=[rasb[a2]])
                fw.op('dve', TT(wa[a2][:, 0:G], asb[a2][:, 0:G], Wv[:, i, 0:G], ALU.mult), reads=[rasb[a2], rW], writes=[rwa[a2]])
                if prev is not None:
                    emit_y(*prev)
                prev = (i, a2, vb, rvb)
                if nxt is not None and i >= 4:
                    next(nxt, None)
            emit_y(*prev)
            if nxt is not None:
                for _ in nxt:
                    pass
            for tt in range(NTT):
                for dh in range(2):
                    yb, ryb = ybanks[tt * 2 + dh]
                    fw.op('act', ACP(ysb[:, tt, dh * 512:(dh + 1) * 512], yb[:, :]), reads=[ryb], writes=[rW])
                for half in range(2):
                    bk, rb = nb(5, 7)
                    for k4 in range(4):
                        m = half * 4 + k4
                        fw.op('pe', TRN(bk[:, k4 * 128:(k4 + 1) * 128], ysb[:, tt, m * 128:(m + 1) * 128], ident_f[:]), reads=[rW, rC], writes=[rb], inc=(k4 == 3))
                    for k4 in range(4):
                        m = half * 4 + k4
                        resid_add(c0 + tt * 128, 128, m, bk[:, k4 * 128:(k4 + 1) * 128], rb)

        groups = []
        c0 = c_peer
        while c0 < NT:
            G = min(256, NT - c0)
            groups.append((c0, G))
            c0 += G
        for _ in route_gen(*groups[0]):
            pass
        for gi, (c0, G) in enumerate(groups):
            nxt = route_gen(*groups[gi + 1]) if gi + 1 < len(groups) else None
            expert_group(c0, G, nxt)
            issue_conv(l + 1, 4)
        issue_conv(l + 1, 1000)
        fw.barrier()

    def final():
        A.reset()
        xn = A.get([8, 128], F32); rxn = Res()
        sq = A.get([8, 128], BF16); rsq = Res()
        rs = A.get([128], F32); rrs = Res()
        ob = [A.get([1024], F32) for _ in range(2)]; rob = [Res() for _ in range(2)]
        n = 0
        for c0 in list(range(OWN0, SMP0, 128)) + [SMP0]:
            norm(c0, 128, 8, xn, rxn, sq, rsq, rs, rrs)
            o = ob[n % 2]
            ro = rob[n % 2]
            n += 1
            for half in range(2):
                bk, rb = nb()
                for k4 in range(4):
                    fw.op('pe', TRN(bk[:, k4 * 128:(k4 + 1) * 128], xn[:, half * 4 + k4, :], ident_f[:]), reads=[rxn, rC], writes=[rb], inc=(k4 == 3))
                fw.op('act', ACP(o[:, half * 512:(half + 1) * 512], bk[:, :]), reads=[rb], writes=[ro])
            if c0 < SMP0:
                fw.dma('sp', yp_d[c0 - OWN0:c0 - OWN0 + 128, :], o, reads=[ro])
            else:
                fw.dma('sp', ys_d[:, :], o[0:64, :], reads=[ro])

    C_KV = [0, None, 256, None]
    C_MIX = [None, 128, None, 384]
    C_PEER = [128, 256, 384, 512]
    for l in range(n_layers):
        j = l // 2
        if l % 2 == 0:
            attn_layer(l, j, C_KV[l])
        else:
            conv_layer(l, j, C_MIX[l])
        if do_peer:
            peer_layer(l, C_PEER[l])
    final()
    fw.barrier()
    fw.emit()
    print("instr counts", {e: len(fw.q[e]) for e in ENGS})
    return nc


def _t5_bucket(dist):
    n = np.maximum(dist, 0)
    nf = np.maximum(n, 1).astype(np.float32)
    large = 16 + (np.log(nf / 16) / np.float32(np.log(128 / 16)) * 16).astype(np.int32)
    large = np.minimum(large, 31)
    return np.where(n < 16, n, large)


def _bucket_onehot():
    c = np.arange(128)[:, None]
    q = np.arange(128)[None, :]
    out = np.zeros((128, 33, 2, 128), np.float32)
    for w, dist in ((0, q - c + 128), (1, q - c)):
        valid = (dist >= 0) & (dist < 128)
        bk = _t5_bucket(dist)
        for b in range(32):
            out[:, b, w, :] = (valid & (bk == b)).astype(np.float32)
        out[:, 32, w, :] = (~valid).astype(np.float32)
    return out


def prep_inputs(x_prompt, x_sample, cache_k, cache_v, state_conv, norm_mix_g, norm_ffn_g, norm_final_g,
                rel_bias, attn_w_qkv, attn_sinks, attn_w_o, conv_w_in, conv_w, conv_w_out,
                peer_w_q, peer_sub_keys, peer_u, peer_v):
    f = lambda a: np.ascontiguousarray(np.asarray(a, dtype=np.float32))
    x_prompt, x_sample = f(x_prompt), f(x_sample)
    cache_k, cache_v, state_conv = f(cache_k), f(cache_v), f(state_conv)
    sh = {}
    g = np.concatenate([f(norm_mix_g), f(norm_ffn_g), f(norm_final_g)[None]], 0)
    sh["gains"] = f(g.reshape(9, 8, 128).transpose(2, 0, 1).reshape(128, 72))
    relb = np.concatenate([f(rel_bias), np.full((1, 16), NEG, np.float32)], 0).reshape(1, 33 * 16)
    sh["relb"] = f(np.repeat(relb, 128, 0))
    sh["ohb"] = f(_bucket_onehot().reshape(128, 8, 1056))
    sh["sinks"] = f(np.repeat(f(attn_sinks).reshape(1, 32), 64, 0))
    sh["wqkv"] = f(f(attn_w_qkv).reshape(2, 8, 128, 1536).transpose(0, 2, 1, 3))
    sh["wo"] = f(f(attn_w_o).reshape(2, 16, 64, 1024).transpose(0, 2, 1, 3))
    sh["win"] = f(f(conv_w_in).reshape(2, 8, 128, 3072).transpose(0, 2, 1, 3).reshape(2, 128, 24, 1024))
    sh["wout"] = f(f(conv_w_out).reshape(2, 8, 128, 1024).transpose(0, 2, 1, 3))
    sh["convw"] = f(f(conv_w).reshape(2, 3, 8, 128).transpose(3, 0, 1, 2))
    sh["wq"] = f(f(peer_w_q).reshape(4, 8, 128, 16, 128).transpose(0, 3, 2, 1, 4).reshape(4, 16, 128, 1024))
    sh["skt"] = f(f(peer_sub_keys).reshape(4, 16, 128, 128).transpose(0, 3, 1, 2).reshape(4, 128, 2048))
    sh["ut"] = f(f(peer_u).reshape(4, 128, 128, 8, 128).transpose(0, 1, 4, 3, 2).reshape(4, 128, 128, 1024))
    sh["vt"] = f(f(peer_v).reshape(4, 128, 128, 1024))
    maps = []
    for c in range(8):
        b, half = c // 2, c % 2
        s0 = half * 2048
        cols = np.zeros((NT, 1024), np.float32)
        if half == 1:
            cols[0:512] = x_prompt[b, s0 - 512:s0]
        cols[512:2560] = x_prompt[b, s0:s0 + 2048]
        cols[2560:2624] = x_sample[16 * c:16 * c + 16].reshape(64, 1024)
        m = dict(sh)
        m["xT"] = f(cols.reshape(NT, 8, 128).transpose(2, 1, 0))
        m["flag"] = np.full((128, 1), NEG if half == 0 else 0.0, np.float32)
        m["ck"] = f(cache_k[:, 16 * c:16 * c + 16].reshape(2, 16, 128, 256))
        m["cv"] = f(cache_v[:, 16 * c:16 * c + 16].reshape(2, 16, 128, 256))
        m["stT"] = f(state_conv[:, 16 * c:16 * c + 16].reshape(2, 16, 2, 8, 128).transpose(4, 0, 3, 1, 2))
        maps.append(m)
    return maps


def assemble(results):
    yp = np.zeros((4, 4096, 1024), np.float32)
    ys = np.zeros((128, 4, 1024), np.float32)
    kp = np.zeros((2, 4, 128, 4, 64), np.float32)
    vp = np.zeros((2, 4, 128, 4, 64), np.float32)
    cp = np.zeros((2, 4, 2, 1024), np.float32)
    ksn = np.zeros((2, 128, 128, 4, 64), np.float32)
    vsn = np.zeros((2, 128, 128, 4, 64), np.float32)
    csn = np.zeros((2, 128, 2, 1024), np.float32)
    for c in range(8):
        r = results[c]
        b, half = c // 2, c % 2
        yp[b, half * 2048:(half + 1) * 2048] = r["yp"]
        ys[16 * c:16 * c + 16] = r["ysm"].reshape(16, 4, 1024)
        if half == 1:
            kp[:, b] = r["kvp"][:, :, 0:256].reshape(2, 128, 4, 64)
            vp[:, b] = r["kvp"][:, :, 256:512].reshape(2, 128, 4, 64)
            cp[:, b] = r["cp"]
        kw = np.concatenate([r["ksn"][:, :, 4:128, :], r["knw"][:, :, :, 0:256]], axis=2)
        vw = np.concatenate([r["vsn"][:, :, 4:128, :], r["knw"][:, :, :, 256:512]], axis=2)
        ksn[:, 16 * c:16 * c + 16] = kw.reshape(2, 16, 128, 4, 64)
        vsn[:, 16 * c:16 * c + 16] = vw.reshape(2, 16, 128, 4, 64)
        csn[:, 16 * c:16 * c + 16] = r["csn"].reshape(2, 16, 2, 1024)
    return (yp, ys, kp, vp, cp, ksn, vsn, csn)


def kernel(**inputs):
    maps = prep_inputs(**inputs)
    nc = build_program()
    res = run_bass_kernel_spmd(nc, maps, core_ids=list(range(8)))
    return assemble(res.results)
```

```python
import contextlib
import numpy as np
import concourse.bass as bass
import concourse.mybir as mybir
from concourse.bass_utils import run_bass_kernel_spmd

F32 = mybir.dt.float32
BF16 = mybir.dt.bfloat16
U32 = mybir.dt.uint32
ALU = mybir.AluOpType
AF = mybir.ActivationFunctionType
AX = mybir.AxisListType

ENGS = ['pe', 'act', 'dve', 'pool', 'sp']
NSLOT = 16
NT = 2688
OWN0 = 512
SMP0 = 2560
NEG = -1e30


class Res:
    __slots__ = ('w', 'r')

    def __init__(self):
        self.w = None
        self.r = {}


class FW:
    def __init__(self, nc):
        self.nc = nc
        self.es = contextlib.ExitStack()
        self.q = {e: [] for e in ENGS}
        self.cnt = {e: 0 for e in ENGS}
        self.sems = {}
        for e in ENGS:
            self.sems[e] = self.es.enter_context(nc.semaphore('s_' + e))
        self.seen = {e: {} for e in ENGS}
        self.dslot = {}
        self.dval = {}
        for qn in ('sp', 'act', 'pool'):
            self.dslot[qn] = 0
            for s in range(NSLOT):
                k = ('d', qn, s)
                self.sems[k] = self.es.enter_context(nc.semaphore('d_%s%d' % (qn, s)))
                self.dval[k] = 0
        self.ntens = 0

    def sb(self, shape, dtype):
        self.ntens += 1
        return self.es.enter_context(self.nc.sbuf_tensor('t%d' % self.ntens, list(shape), dtype))

    def ps(self, shape, dtype):
        self.ntens += 1
        return self.es.enter_context(self.nc.psum_tensor('p%d' % self.ntens, list(shape), dtype))

    def _need(self, reads, writes):
        need = {}
        for r in reads:
            if r.w is not None and r.w[1] > need.get(r.w[0], 0):
                need[r.w[0]] = r.w[1]
        for w in writes:
            if w.w is not None and w.w[1] > need.get(w.w[0], 0):
                need[w.w[0]] = w.w[1]
            for k, v in w.r.items():
                if v > need.get(k, 0):
                    need[k] = v
        return need

    def _waits(self, eng, need):
        wl = []
        seen = self.seen[eng]
        for k, v in need.items():
            if eng == 'pe' and k == 'pe':
                continue
            if v > seen.get(k, 0):
                seen[k] = v
                wl.append((k, v))
        return wl

    def op(self, eng, fn, reads=(), writes=(), inc=True):
        wl = self._waits(eng, self._need(reads, writes))
        tgt = self.cnt[eng] + 1
        if inc:
            self.cnt[eng] = tgt
        self.q[eng].append((wl, fn, inc))
        for r in reads:
            if tgt > r.r.get(eng, 0):
                r.r[eng] = tgt
        for w in writes:
            w.w = (eng, tgt)
            w.r = {}

    def dma(self, qn, out, in_, reads=(), writes=(), **kw):
        slot = self.dslot[qn] % NSLOT
        self.dslot[qn] += 1
        key = ('d', qn, slot)
        prev = self.dval[key]
        tgt = prev + 16
        self.dval[key] = tgt
        need = self._need(reads, writes)
        if prev > 0:
            need[key] = max(need.get(key, 0), prev)
        wl = self._waits(qn, need)
        sem = self.sems[key]

        def fn(e, out=out, in_=in_, kw=kw, sem=sem):
            e.dma_start(out=out, in_=in_, **kw).then_inc(sem, 16)
            return None
        self.q[qn].append((wl, fn, False))
        for r in reads:
            r.r[key] = tgt
        for w in writes:
            w.w = (key, tgt)
            w.r = {}

    def barrier(self):
        need = {}
        for k, v in self.dval.items():
            if v > 0:
                need[k] = v
        for e in ENGS:
            if self.cnt[e] > 0:
                need[e] = self.cnt[e]
        for e in ENGS:
            nd = {k: v for k, v in need.items() if k != e}
            wl = self._waits(e, nd)
            if wl:
                self.q[e].append((wl, None, False))

    def emit(self):
        nc = self.nc
        sems = self.sems

        def mk(e):
            def body(engobj):
                mysem = sems[e]
                for wl, fn, inc in self.q[e]:
                    for k, v in wl:
                        engobj.wait_ge(sems[k], v)
                    if fn is None:
                        continue
                    ins = fn(engobj)
                    if inc:
                        ins.then_inc(mysem, 1)
            return body
        with nc.Block() as block:
            block.tensor(mk('pe'))
            block.scalar(mk('act'))
            block.vector(mk('dve'))
            block.gpsimd(mk('pool'))
            block.sync(mk('sp'))


class Arena:
    def __init__(self, t, nbytes):
        self.t = t
        self.n = nbytes
        self.off = 0

    def reset(self):
        self.off = 0

    def get(self, shape, dtype, parts=128):
        shape = list(shape)
        n = int(np.prod(shape))
        es = 2 if dtype == BF16 else 4
        nb = n * es
        o = self.off
        self.off += (nb + 63) // 64 * 64
        assert self.off <= self.n, ("arena overflow", self.off, self.n)
        v = self.t[0:parts, o // 2:(o + nb) // 2]
        if es == 4:
            v = v.bitcast(dtype)
        if len(shape) > 1:
            names = ['d%d' % i for i in range(len(shape))]
            pat = "p (" + " ".join(names) + ") -> p " + " ".join(names)
            v = v.rearrange(pat, **{names[i]: shape[i] for i in range(len(shape))})
        return v


def MM(out, lhsT, rhs, start, stop):
    return lambda e: e.matmul(out, lhsT=lhsT, rhs=rhs, start=start, stop=stop)


def TRN(out, in_, ident):
    return lambda e: e.transpose(out=out, in_=in_, identity=ident)


def ACTF(out, in_, func, **kw):
    return lambda e: e.activation(out=out, in_=in_, func=func, **kw)


def ACP(out, in_):
    return lambda e: e.copy(out=out, in_=in_)


def TT(out, in0, in1, op):
    return lambda e: e.tensor_tensor(out=out, in0=in0, in1=in1, op=op)


def TS(out, in0, s1, s2, op0, op1=None):
    if op1 is None:
        return lambda e: e.tensor_scalar(out=out, in0=in0, scalar1=s1, scalar2=None, op0=op0)
    return lambda e: e.tensor_scalar(out=out, in0=in0, scalar1=s1, scalar2=s2, op0=op0, op1=op1)


def STT(out, in0, scalar, in1, op0, op1):
    return lambda e: e.scalar_tensor_tensor(out=out, in0=in0, scalar=scalar, in1=in1, op0=op0, op1=op1)


def TC(out, in_):
    return lambda e: e.tensor_copy(out=out, in_=in_)


def RCP(out, in_):
    return lambda e: e.reciprocal(out=out, in_=in_)


def RED(out, in_, op):
    return lambda e: e.tensor_reduce(out=out, in_=in_, axis=AX.X, op=op)


def MSET(out, v):
    return lambda e: e.memset(out, v)


def MAX8(out, in_):
    return lambda e: e.max(out=out, in_=in_)


def MIDX(out, in_max, in_values):
    return lambda e: e.max_index(out=out, in_max=in_max, in_values=in_values)


def MREP(out, rep, vals):
    return lambda e: e.match_replace(out=out, in_to_replace=rep, in_values=vals, imm_value=NEG)


def build_program(n_layers=4, do_peer=True, sample_attn=True, dbg=9):
    nc = bass.Bass("TRN2", target_bir_lowering=False)
    fw = FW(nc)

    def din(name, shape, dt=F32):
        return nc.dram_tensor(name, list(shape), dt, kind="ExternalInput").ap()

    def dout(name, shape, dt=F32):
        return nc.dram_tensor(name, list(shape), dt, kind="ExternalOutput").ap()

    xT_d = din("xT", [128, 8, NT])
    flag_d = din("flag", [128, 1])
    ck_d = din("ck", [2, 16, 128, 256])
    cv_d = din("cv", [2, 16, 128, 256])
    stT_d = din("stT", [128, 2, 8, 16, 2])
    gains_d = din("gains", [128, 72])
    relb_d = din("relb", [128, 33 * 16])
    ohb_d = din("ohb", [128, 8, 1056])
    sinks_d = din("sinks", [64, 32])
    wqkv_d = din("wqkv", [2, 128, 8, 1536])
    wo_d = din("wo", [2, 64, 16, 1024])
    win_d = din("win", [2, 128, 24, 1024])
    wout_d = din("wout", [2, 128, 8, 1024])
    convw_d = din("convw", [128, 2, 3, 8])
    NLP = 4 if do_peer else 1
    wq_d = din("wq", [NLP, 16, 128, 1024])
    skt_d = din("skt", [4, 128, 2048])
    ut_d = din("ut", [NLP, 128 if do_peer else 1, 128, 1024])
    vt_d = din("vt", [NLP, 128 if do_peer else 1, 128, 1024])

    ubf = nc.dram_tensor("ubf", [NLP, 128 if do_peer else 1, 128, 1024], BF16, kind="Internal").ap()
    vbf = nc.dram_tensor("vbf", [NLP, 128 if do_peer else 1, 128, 1024], BF16, kind="Internal").ap()
    wqbf = nc.dram_tensor("wqbf", [NLP, 16, 128, 1024], BF16, kind="Internal").ap()
    rcU = [[Res() for _ in range(16)] for _ in range(4)]
    rcV = [[Res() for _ in range(16)] for _ in range(4)]
    rcQ = [[Res() for _ in range(2)] for _ in range(4)]
    cjobs = {}
    for l_ in range(4 if do_peer else 0):
        jl = []
        for b_ in range(2):
            jl.append((wqbf[l_, 8 * b_:8 * b_ + 8], wq_d[l_, 8 * b_:8 * b_ + 8], rcQ[l_][b_]))
        for b_ in range(16):
            jl.append((ubf[l_, 8 * b_:8 * b_ + 8], ut_d[l_, 8 * b_:8 * b_ + 8], rcU[l_][b_]))
            jl.append((vbf[l_, 8 * b_:8 * b_ + 8], vt_d[l_, 8 * b_:8 * b_ + 8], rcV[l_][b_]))
        cjobs[l_] = jl

    def issue_conv(l_, n):
        jl = cjobs.get(l_)
        while jl and n > 0:
            dst, src, r_ = jl.pop(0)
            fw.dma('pool', dst, src, writes=[r_])
            n -= 1

    yp_d = dout("yp", [2048, 1024])
    ys_d = dout("ysm", [64, 1024])
    kvp_d = dout("kvp", [2, 128, 512])
    knw_d = dout("knw", [2, 16, 4, 512])
    cp_d = dout("cp", [2, 2, 1024])
    ksn_d = dout("ksn", [2, 16, 128, 256])
    vsn_d = dout("vsn", [2, 16, 128, 256])
    csn_d = dout("csn", [2, 32, 1024])

    hT = fw.sb([128, 8, NT], F32)
    rH = [Res() for _ in range(NT // 128)]

    def rh(c0, c1):
        return rH[c0 // 128:(c1 + 127) // 128]

    Wreg = fw.sb([128, 32768], BF16)
    rW = Res()
    gains = fw.sb([128, 72], F32); rG = Res()
    convw = fw.sb([128, 2, 3, 8], F32)
    flag = fw.sb([128, 1], F32)
    epsb = fw.sb([128, 1], F32)
    ident_f = fw.sb([128, 128], F32)
    ident_b = fw.sb([128, 128], BF16)
    ones_b = fw.sb([128, 128], BF16)
    iota_b = fw.sb([128, 128], BF16)
    iota16 = fw.sb([128, 16], F32)
    thr16 = fw.sb([128, 16], F32)
    sinkexp = fw.sb([64, 32], F32)
    tbl = fw.sb([128, 3, 16, 128], BF16)
    skt = fw.sb([128, 16, 128], BF16); rSK = Res()
    rC = Res()
    ARENA_BYTES = 42752
    arena_t = fw.sb([128, ARENA_BYTES // 2], BF16)
    A = Arena(arena_t, ARENA_BYTES)
    banks = [fw.ps([128, 512], F32) for _ in range(8)]
    rB = [Res() for _ in range(8)]
    bctr = [0]

    def nb(lo=0, hi=8):
        i = lo + bctr[0] % (hi - lo)
        bctr[0] += 1
        return banks[i], rB[i]

    for kc in range(8):
        fw.dma('sp', hT[:, kc, :], xT_d[:, kc, :], writes=rH)
    fw.dma('sp', gains[:], gains_d[:, :], writes=[rG])
    fw.dma('sp', convw[:], convw_d[:, :, :, :], writes=[rC])
    fw.dma('sp', flag[:], flag_d[:, :], writes=[rC])
    fw.dma('sp', sinkexp[:], sinks_d[:, :], writes=[rC])
    fw.op('pool', MSET(ident_f[:], 0.0), writes=[rC])
    fw.op('pool', lambda e: e.affine_select(out=ident_f[:], in_=ident_f[:], pattern=[[-1, 128]], compare_op=ALU.not_equal,
                                            fill=1.0, base=0, channel_multiplier=1), reads=[rC], writes=[rC])
    fw.op('pool', TC(ident_b[:], ident_f[:]), reads=[rC], writes=[rC])
    fw.op('pool', MSET(ones_b[:], 1.0), writes=[rC])
    fw.op('pool', MSET(epsb[:], 1e-6), writes=[rC])
    fw.op('pool', lambda e: e.iota(iota_b[:], pattern=[[1, 128]], base=0, channel_multiplier=0,
                                   allow_small_or_imprecise_dtypes=True), writes=[rC])
    fw.op('pool', lambda e: e.iota(iota16[:], pattern=[[1, 16]], base=0, channel_multiplier=0,
                                   allow_small_or_imprecise_dtypes=True), writes=[rC])
    fw.op('pool', lambda e: e.iota(thr16[:], pattern=[[16, 16]], base=16, channel_multiplier=0,
                                   allow_small_or_imprecise_dtypes=True), writes=[rC])
    fw.op('act', ACTF(sinkexp[:], sinkexp[:], AF.Exp), reads=[rC], writes=[rC])
    def build_tbl():
        A.reset()
        ohb = A.get([33, 2, 128], BF16)
        relb = A.get([33 * 16], F32)
        acc = A.get([2, 16, 128], F32)
        rS = Res()
        fw.dma('pool', ohb.rearrange("p b w q -> p (b w q)").rearrange("p (a x) -> p a x", a=8), ohb_d[:, :, :], writes=[rS])
        fw.dma('sp', relb, relb_d[:, :], writes=[rS])
        fw.op('dve', TS(relb[:, 0:512], relb[:, 0:512], 8.0, None, ALU.mult), reads=[rS], writes=[rS])
        for h in range(16):
            for b in range(33):
                sc1 = relb[:, b * 16 + h:b * 16 + h + 1]
                if b == 0:
                    fw.op('dve', TS(acc[:, :, h, :], ohb[:, b, :, :], sc1, None, ALU.mult), reads=[rS], writes=[rS])
                else:
                    fw.op('dve', STT(acc[:, :, h, :], ohb[:, b, :, :], sc1, acc[:, :, h, :], ALU.mult, ALU.add), reads=[rS], writes=[rS])
        fw.op('dve', TC(tbl[:, 0:2, :, :], acc), reads=[rS], writes=[rC])
        fw.op('dve', TS(tbl[:, 2, :, :], acc[:, 0, :, :], flag[:, 0:1], None, ALU.add), reads=[rS, rC], writes=[rC])
        fw.barrier()

    fw.barrier()

    def norm(c0, C, gi, xn, rxn, sq, rsq, rs, rrs, blo=0, bhi=8):
        hr = rh(c0, c0 + C)
        fw.op('act', ACTF(sq[:, :, 0:C], hT[:, :, c0:c0 + C], AF.Square), reads=hr, writes=[rsq])
        bk, rb = nb(blo, bhi)
        for kc in range(8):
            fw.op('pe', MM(bk[:, 0:C], ones_b[:], sq[:, kc, 0:C], kc == 0, kc == 7), reads=[rsq, rC], writes=[rb], inc=(kc == 7))
        fw.op('act', ACTF(rs[:, 0:C], bk[:, 0:C], AF.Sqrt, scale=1.0 / 1024.0, bias=epsb[:, 0:1]), reads=[rb, rC], writes=[rrs])
        fw.op('dve', RCP(rs[:, 0:C], rs[:, 0:C]), reads=[rrs], writes=[rrs])
        for kc in range(8):
            fw.op('dve', STT(xn[:, kc, 0:C], hT[:, kc, c0:c0 + C], gains[:, gi * 8 + kc:gi * 8 + kc + 1], rs[:, 0:C], ALU.mult, ALU.mult),
                  reads=hr + [rrs, rG], writes=[rxn])

    def proj(wfn, M, nch, xn, rxn, C, evac, wres):
        for m in range(nch):
            bk, rb = nb()
            for kc in range(8):
                fw.op('pe', MM(bk[0:M, 0:C], wfn(kc, m), xn[:, kc, 0:C], kc == 0, kc == 7), reads=[wres, rxn], writes=[rb], inc=(kc == 7))
            evac(m, bk, rb)

    def resid_add(c0, C, m, src, rb):
        fw.op('dve', TT(hT[:, m, c0:c0 + C], src, hT[:, m, c0:c0 + C], ALU.add), reads=[rb] + rh(c0, c0 + C), writes=rh(c0, c0 + C))

    def hq(ap, h=4):
        return ap.rearrange("p (h q) -> p h q", h=h)

    def attn_layer(l, j, c_kv):
        build_tbl()
        A.reset()
        wqkv = Wreg[:, 0:12288].rearrange("p (k n) -> p k n", k=8)
        wo = Wreg[0:64, 12288:12288 + 16384].rearrange("p (h n) -> p h n", h=16)
        fw.dma('pool', wqkv, wqkv_d[j], writes=[rW])
        fw.dma('pool', wo, wo_d[j], writes=[rW])
        if l == 0:
            issue_conv(0, 1000)
        xn = A.get([8, 128], BF16); rxn = Res()
        sq = A.get([8, 128], BF16); rsq = Res()
        rs = A.get([128], F32); rrs = Res()
        QT = A.get([16, 128], BF16, parts=64); rQ = Res()
        KT = A.get([4, 256], BF16, parts=64); rK = Res()
        V3 = A.get([2, 256], BF16); rV = Res()
        AT = A.get([16, 128], BF16, parts=64); rA = Res()
        PT = [A.get([4, 128], BF16) for _ in range(4)]; rP = [Res() for _ in range(4)]
        tmp = A.get([512], F32, parts=64); rT = Res()
        kvf = A.get([512], F32); rKV = Res()
        kst = A.get([4, 256], F32); rKS = Res()
        KTc = A.get([4, 4, 128], BF16, parts=64); rKC = Res()
        Vc = A.get([4, 256], BF16); rVC = Res()
        Vn = A.get([4, 256], BF16, parts=4); rVN = Res()
        knew = [A.get([512], F32, parts=4) for _ in range(2)]; rKN = [Res() for _ in range(2)]
        print("attn arena", A.off)
        pctr = [0]

        def qkv_chunk(c0, C):
            norm(c0, C, l, xn, rxn, sq, rsq, rs, rrs)

            def ev_q(m, bk, rb):
                fw.op('act', ACP(QT[:, m, 0:C], bk[0:64, 0:C]), reads=[rb], writes=[rQ])
            proj(lambda kc, m: wqkv[:, kc, m * 64:(m + 1) * 64], 64, 16, xn, rxn, C, ev_q, rW)

            def ev_k(m, bk, rb):
                fw.op('act', ACP(KT[:, m, 128:128 + C], bk[0:64, 0:C]), reads=[rb], writes=[rK])
            if dbg >= 0.25:
                proj(lambda kc, m: wqkv[:, kc, 1024 + m * 64:1024 + (m + 1) * 64], 64, 4, xn, rxn, C, ev_k, rW)

        def kv_tok(blk, is_last_own):
            bk, rb = nb()
            for kc in range(8):
                fw.op('pe', MM(bk[:, :], xn[:, kc, blk * 128:(blk + 1) * 128], wqkv[:, kc, 1024:1536], kc == 0, kc == 7),
                      reads=[rxn, rW], writes=[rb], inc=(kc == 7))
            fw.op('act', ACP(V3[:, 1 + blk, :], bk[:, 256:512]), reads=[rb], writes=[rV])
            if is_last_own:
                fw.op('act', ACP(kvf, bk[:, :]), reads=[rb], writes=[rKV])
                fw.dma('sp', kvp_d[j], kvf, reads=[rKV])

        def attn_block(blk, first):
            wsel = 2 if first else 0
            for g in range(4):
                sp_, rsp = nb()
                sc_, rsc = nb()
                qv = QT[:, 4 * g:4 * g + 4, blk * 128:(blk + 1) * 128]
                fw.op('pe', MM(hq(sp_[:, :]), KT[:, g, blk * 128:(blk + 1) * 128], qv, True, False), reads=[rK, rQ], writes=[rsp], inc=False)
                fw.op('pe', MM(hq(sp_[:, :]), ident_b[:], tbl[:, wsel, 4 * g:4 * g + 4, :], False, True), reads=[rC], writes=[rsp])
                fw.op('pe', MM(hq(sc_[:, :]), KT[:, g, (blk + 1) * 128:(blk + 2) * 128], qv, True, False), reads=[rK, rQ], writes=[rsc], inc=False)
                fw.op('pe', MM(hq(sc_[:, :]), ident_b[:], tbl[:, 1, 4 * g:4 * g + 4, :], False, True), reads=[rC], writes=[rsc])
                ia = pctr[0] % 2
                pctr[0] += 1
                Pp, rPp = PT[2 * ia], rP[2 * ia]
                Pc, rPc = PT[2 * ia + 1], rP[2 * ia + 1]
                fw.op('act', ACTF(Pp, hq(sp_[:, :]), AF.Exp, scale=0.125), reads=[rsp], writes=[rPp])
                fw.op('act', ACTF(Pc, hq(sc_[:, :]), AF.Exp, scale=0.125), reads=[rsc], writes=[rPc])
                ob, rob = nb()
                db, rdb = nb()
                fw.op('pe', MM(hq(ob[0:64, :]), V3[:, blk, g * 64:(g + 1) * 64], Pp, True, False), reads=[rV, rPp], writes=[rob], inc=False)
                fw.op('pe', MM(hq(ob[0:64, :]), V3[:, blk + 1, g * 64:(g + 1) * 64], Pc, False, True), reads=[rV, rPc], writes=[rob])
                fw.op('pe', MM(hq(db[0:64, :]), ones_b[:, 0:64], Pp, True, False), reads=[rPp, rC], writes=[rdb], inc=False)
                fw.op('pe', MM(hq(db[0:64, :]), ones_b[:, 0:64], Pc, False, True), reads=[rPc, rC], writes=[rdb])
                sk = sinkexp[:, j * 16 + 4 * g:j * 16 + 4 * g + 4].unsqueeze(2).to_broadcast([64, 4, 128])
                tv = hq(tmp)
                fw.op('dve', TT(tv, hq(db[0:64, :]), sk, ALU.add), reads=[rdb, rC], writes=[rT])
                fw.op('dve', RCP(tmp, tmp), reads=[rT], writes=[rT])
                fw.op('dve', TT(AT[:, 4 * g:4 * g + 4, blk * 128:(blk + 1) * 128], hq(ob[0:64, :]), tv, ALU.mult), reads=[rob, rT], writes=[rA])

        def out_proj(c0, a0, C):
            for m in range(8):
                bk, rb = nb()
                for hh in range(16):
                    fw.op('pe', MM(bk[:, 0:C - a0], wo[:, hh, m * 128:(m + 1) * 128], AT[:, hh, a0:C], hh == 0, hh == 15),
                          reads=[rW, rA], writes=[rb], inc=(hh == 15))
                resid_add(c0 + a0, C - a0, m, bk[:, 0:C - a0], rb)

        def roll():
            fw.op('dve', TC(KT[:, :, 0:128], KT[:, :, 128:256]), reads=[rK], writes=[rK])
            fw.op('dve', TC(V3[:, 0, :], V3[:, 1, :]), reads=[rV], writes=[rV])

        c0 = c_kv
        firstchunk = True
        while c0 < SMP0:
            if dbg < 0.15:
                break
            qkv_chunk(c0, 128)
            if dbg >= 0.3:
                kv_tok(0, is_last_own=(c0 == SMP0 - 128) and dbg >= 0.5)
            if not firstchunk and dbg >= 2:
                bidx = c0 // 128 - 4
                attn_block(0, first=(bidx <= 0))
                if dbg >= 3:
                    out_proj(c0, 0, 128)
            if dbg >= 0.4:
                roll()
            firstchunk = False
            c0 += 128
        if not sample_attn:
            fw.barrier()
            return
        c0 = SMP0
        qkv_chunk(c0, 128)
        fw.op('dve', MSET(AT[:, :, 0:128], 0.0), writes=[rA])
        Ppv = PT[0].rearrange("p a b -> p (a b)")
        Pcv = PT[1].rearrange("p a b -> p (a b)")
        rPp, rPc = rP[0], rP[1]
        for s4 in range(4):
            fw.dma('sp', kst, cv_d[j, 4 * s4:4 * s4 + 4].rearrange("s c n -> c s n"), writes=[rKS])
            fw.op('act', ACP(Vc, kst), reads=[rKS], writes=[rVC])
            fw.dma('sp', vsn_d[j, 4 * s4:4 * s4 + 4].rearrange("s c n -> c s n"), kst, reads=[rKS])
            fw.dma('sp', kst, ck_d[j, 4 * s4:4 * s4 + 4].rearrange("s c n -> c s n"), writes=[rKS])
            fw.dma('sp', ksn_d[j, 4 * s4:4 * s4 + 4].rearrange("s c n -> c s n"), kst, reads=[rKS])
            for si in range(4):
                s = 4 * s4 + si
                bk, rb = nb()
                for kc in range(8):
                    fw.op('pe', MM(bk[0:4, :], xn[:, kc, 4 * s:4 * s + 4], wqkv[:, kc, 1024:1536], kc == 0, kc == 7),
                          reads=[rxn, rW], writes=[rb], inc=(kc == 7))
                fw.op('act', ACP(Vn[:, si, :], bk[0:4, 256:512]), reads=[rb], writes=[rVN])
                kn, rkn = knew[s % 2], rKN[s % 2]
                fw.op('act', ACP(kn, bk[0:4, :]), reads=[rb], writes=[rkn])
                fw.dma('sp', knw_d[j, s], kn, reads=[rkn])
                bk, rb = nb()
                for g in range(4):
                    fw.op('pe', TRN(bk[0:64, g * 128:(g + 1) * 128], kst[:, si, g * 64:(g + 1) * 64], ident_f[:]),
                          reads=[rKS, rC], writes=[rb], inc=(g == 3))
                fw.op('act', ACP(KTc[:, si, :, :], bk[0:64, :].rearrange("p (g c) -> p g c", g=4)), reads=[rb], writes=[rKC])
            sp_, rsp = nb()
            sc_, rsc = nb()
            for si in range(4):
                s = 4 * s4 + si
                for g in range(4):
                    qv = QT[:, 4 * g:4 * g + 4, 4 * s:4 * s + 4]
                    o0 = si * 64 + g * 16
                    last = (si == 3 and g == 3)
                    fw.op('pe', MM(hq(sp_[:, o0:o0 + 16]), KTc[:, si, g, :], qv, True, False), reads=[rKC, rQ], writes=[rsp], inc=False)
                    fw.op('pe', MM(hq(sp_[:, o0:o0 + 16]), ident_b[:], tbl[:, 0, 4 * g:4 * g + 4, 0:4], False, True), reads=[rC], writes=[rsp], inc=False)
                    fw.op('pe', MM(hq(sc_[0:4, o0:o0 + 16]), KT[:, g, 128 + 4 * s:128 + 4 * s + 4], qv, True, False), reads=[rK, rQ], writes=[rsc], inc=False)
                    fw.op('pe', MM(hq(sc_[0:4, o0:o0 + 16]), ident_b[0:4, 0:4], tbl[0:4, 1, 4 * g:4 * g + 4, 0:4], False, True),
                          reads=[rC], writes=[rsp, rsc], inc=last)
            fw.op('act', ACTF(Ppv[:, 0:256], sp_[:, 0:256], AF.Exp, scale=0.125), reads=[rsp], writes=[rPp])
            fw.op('act', ACTF(Pcv[0:4, 0:256], sc_[0:4, 0:256], AF.Exp, scale=0.125), reads=[rsc], writes=[rPc])
            ob, rob = nb()
            db, rdb = nb()
            for si in range(4):
                for g in range(4):
                    o0 = si * 64 + g * 16
                    fw.op('pe', MM(ob[0:64, o0:o0 + 16], Vc[:, si, g * 64:(g + 1) * 64], Ppv[:, o0:o0 + 16], True, False), reads=[rVC, rPp], writes=[rob], inc=False)
                    fw.op('pe', MM(ob[0:64, o0:o0 + 16], Vn[:, si, g * 64:(g + 1) * 64], Pcv[0:4, o0:o0 + 16], False, True), reads=[rVN, rPc], writes=[rob], inc=False)
                fw.op('pe', MM(db[0:64, si * 64:(si + 1) * 64], ones_b[:, 0:64], Ppv[:, si * 64:(si + 1) * 64], True, False), reads=[rPp, rC], writes=[rdb], inc=False)
                fw.op('pe', MM(db[0:64, si * 64:(si + 1) * 64], ones_b[0:4, 0:64], Pcv[0:4, si * 64:(si + 1) * 64], False, True),
                      reads=[rPc, rC], writes=[rdb, rob], inc=(si == 3))
            sk = sinkexp[:, j * 16:j * 16 + 16].unsqueeze(1).unsqueeze(3).to_broadcast([64, 4, 16, 4])
            tv = tmp[:, 0:256].rearrange("p (s h q) -> p s h q", s=4, h=16)
            fw.op('dve', TT(tv, db[0:64, 0:256].rearrange("p (s h q) -> p s h q", s=4, h=16), sk, ALU.add), reads=[rdb, rC], writes=[rT])
            fw.op('dve', RCP(tmp[:, 0:256], tmp[:, 0:256]), reads=[rT], writes=[rT])
            fw.op('dve', TT(AT[:, :, 16 * s4:16 * s4 + 16].rearrange("p h (s q) -> p s h q", s=4),
                            ob[0:64, 0:256].rearrange("p (s h q) -> p s h q", s=4, h=16), tv, ALU.mult), reads=[rob, rT], writes=[rA])
        out_proj(c0, 0, 128)
        fw.barrier()

    def conv_layer(l, j, c_mix):
        A.reset()
        win = Wreg[:, 0:24576].rearrange("p (k n) -> p k n", k=8)
        wout = Wreg[:, 24576:32768].rearrange("p (k n) -> p k n", k=8)
        fw.dma('pool', Wreg[:, 0:24576].rearrange("p (a b) -> p a b", a=24), win_d[j], writes=[rW])
        fw.dma('pool', wout, wout_d[j], writes=[rW])
        xn = A.get([8, 256], BF16); rxn = Res()
        sq = A.get([8, 256], BF16); rsq = Res()
        rs = A.get([256], F32); rrs = Res()
        bT = A.get([8, 256], BF16); rb_ = Res()
        cT = A.get([8, 256], F32); rc_ = Res()
        uT = A.get([8, 258], F32); ru = Res()
        uS = A.get([8, 16, 6], F32); ruS = Res()
        yt = A.get([256], F32); ry = Res()
        gT = A.get([8, 256], BF16); rg = Res()
        ot = A.get([1024], F32, parts=32); rot = Res()
        print("conv arena", A.off)
        fw.op('pool', MSET(uT[:, :, 0:2], 0.0), writes=[ru])
        fw.op('pool', MSET(gT, 0.0), writes=[rg])
        fw.dma('sp', uS[:, :, :, 0:2], stT_d[:, j, :, :, :], writes=[ruS])

        def s16(ap):
            return ap.rearrange("p (s q) -> p s q", s=16)

        def chunk(c0, C, sample):
            norm(c0, C, l, xn, rxn, sq, rsq, rs, rrs)

            def ev(m, bk, rb):
                if m < 8:
                    fw.op('act', ACP(bT[:, m, 0:C], bk[:, 0:C]), reads=[rb], writes=[rb_])
                elif m < 16:
                    fw.op('act', ACP(cT[:, m - 8, 0:C], bk[:, 0:C]), reads=[rb], writes=[rc_])
                else:
                    kc = m - 16
                    if not sample:
                        fw.op('dve', TT(uT[:, kc, 2:2 + C], bk[:, 0:C], cT[:, kc, 0:C], ALU.mult), reads=[rb, rc_], writes=[ru])
                    else:
                        fw.op('dve', TT(uS[:, kc, :, 2:6], s16(bk[:, 0:64]), s16(cT[:, kc, 0:64]), ALU.mult), reads=[rb, rc_], writes=[ruS])
            proj(lambda kc, m: win[:, kc, m * 128:(m + 1) * 128], 128, 24, xn, rxn, C, ev, rW)
            for kc in range(8):
                if not sample:
                    u0, u1, u2 = uT[:, kc, 0:C], uT[:, kc, 1:1 + C], uT[:, kc, 2:2 + C]
                    yv, bv, gv = yt[:, 0:C], bT[:, kc, 0:C], gT[:, kc, 0:C]
                    rr = ru
                else:
                    u0, u1, u2 = uS[:, kc, :, 0:4], uS[:, kc, :, 1:5], uS[:, kc, :, 2:6]
                    yv, bv, gv = s16(yt[:, 0:64]), s16(bT[:, kc, 0:64]), s16(gT[:, kc, 0:64])
                    rr = ruS
                w0 = convw[:, j, 0, kc:kc + 1]
                w1 = convw[:, j, 1, kc:kc + 1]
                w2 = convw[:, j, 2, kc:kc + 1]
                fw.op('dve', TS(yv, u2, w2, None, ALU.mult), reads=[rr, rC], writes=[ry])
                fw.op('dve', STT(yv, u1, w1, yv, ALU.mult, ALU.add), reads=[rr, ry, rC], writes=[ry])
                fw.op('dve', STT(yv, u0, w0, yv, ALU.mult, ALU.add), reads=[rr, ry, rC], writes=[ry])
                fw.op('dve', TT(gv, yv, bv, ALU.mult), reads=[ry, rb_], writes=[rg])

            def ev_o(m, bk, rb):
                resid_add(c0, C, m, bk[:, 0:C], rb)
            proj(lambda kc, m: wout[:, kc, m * 128:(m + 1) * 128], 128, 8, gT, rg, C, ev_o, rW)

        c0 = c_mix
        while c0 < SMP0:
            C = min(256, SMP0 - c0)
            chunk(c0, C, False)
            if c0 + C == SMP0:
                for half in range(2):
                    bk, rb = nb()
                    for k4 in range(4):
                        fw.op('pe', TRN(bk[0:2, k4 * 128:(k4 + 1) * 128], uT[:, half * 4 + k4, C:C + 2], ident_f[:]), reads=[ru, rC], writes=[rb], inc=(k4 == 3))
                    fw.op('dve', TC(ot[0:2, half * 512:(half + 1) * 512], bk[0:2, :]), reads=[rb], writes=[rot])
                fw.dma('sp', cp_d[j], ot[0:2, :], reads=[rot])
            else:
                fw.op('dve', TC(uT[:, :, 0:2], uT[:, :, C:C + 2]), reads=[ru], writes=[ru])
            c0 += C
        chunk(SMP0, 128, True)
        for half in range(2):
            bk, rb = nb()
            for k4 in range(4):
                ytv = yt[:, 0:32].rearrange("p (s r) -> p s r", s=16)
                fw.op('dve', TC(ytv, uS[:, half * 4 + k4, :, 4:6]), reads=[ruS], writes=[ry])
                fw.op('pe', TRN(bk[0:32, k4 * 128:(k4 + 1) * 128], yt[:, 0:32], ident_f[:]), reads=[ry, rC], writes=[rb], inc=True)
            fw.op('dve', TC(ot[0:32, half * 512:(half + 1) * 512], bk[0:32, :]), reads=[rb], writes=[rot])
        fw.dma('sp', csn_d[j], ot[0:32, :], reads=[rot])
        fw.barrier()

    def peer_layer(l, c_peer):
        A.reset()
        Wv = Wreg[:, :].rearrange("p (i t) -> p i t", i=128)
        ysb = Wreg[:, 0:4096].bitcast(F32).rearrange("p (t d) -> p t d", t=2)
        fw.dma('pool', skt[:].rearrange("p c k -> p (c k)"), skt_d[l], writes=[rSK])
        tflat = tbl[:].rearrange("p a b c -> p (a b c)")
        uring = [A.get([8, 128], BF16) for _ in range(2)] + [tflat[:, k * 1024:(k + 1) * 1024].rearrange("p (k j) -> p k j", k=8) for k in range(3)]
        vring = [A.get([1024], BF16) for _ in range(2)] + [tflat[:, k * 1024:(k + 1) * 1024] for k in range(3, 6)]
        NR = 5
        rUR = [Res() for _ in range(NR)]
        rVR = [Res() for _ in range(NR)]
        xn = A.get([8, 256], BF16); rxn = Res()
        xnq = A.get([8, 128], BF16); rxq = Res()
        rs = A.get([256], F32); rrs = Res()
        qT = A.get([16, 128], BF16); rq = Res()
        bigraw = A.get([4096], BF16); rbig = Res()
        big = bigraw.bitcast(F32)
        sq = bigraw[:, 0:2048].rearrange("p (k c) -> p k c", k=8); rsq = rbig
        sv = A.get([16, 16], F32); rsv = Res()
        si = A.get([16, 16], U32); rsi = Res()
        sif = A.get([16, 16], F32); rsif = Res()
        tm = A.get([256], F32); rtm = Res()
        best = A.get([8, 16], F32); rbest = Res()
        pos = A.get([8, 16], U32); rpos = Res()
        posf = A.get([8, 16], F32); paf = A.get([8, 16], F32); pbf = A.get([8, 16], F32); rpp = Res()
        sel = A.get([3, 8, 16], F32); rsel = Res()
        gs = A.get([8], F32); rgs = Res()
        trio = A.get([3, 256], BF16); rtrio = Res()
        NO = 4
        O2 = [A.get([128], BF16) for _ in range(NO)]; rO2 = [Res() for _ in range(NO)]
        O1 = [A.get([128], BF16) for _ in range(NO)]; rO1 = [Res() for _ in range(NO)]
        asb = [A.get([256], BF16) for _ in range(2)]; rasb = [Res() for _ in range(2)]
        wa = [A.get([256], BF16) for _ in range(2)]; rwa = [Res() for _ in range(2)]
        rAB = [rB[4], rB[7]]
        print("peer arena", A.off)
        uctr = [0]
        vctr = [0]
        octr = [0]

        def stream_u(src, rsrc):
            s = uctr[0] % NR
            uctr[0] += 1
            fw.dma('sp', uring[s].rearrange("p k j -> p (k j)"), src, reads=[rsrc], writes=[rUR[s]])
            return uring[s], rUR[s]

        def stream_v(src, rsrc):
            s = vctr[0] % NR
            vctr[0] += 1
            fw.dma('sp', vring[s], src, reads=[rsrc], writes=[rVR[s]])
            return vring[s], rVR[s]

        def route_gen(c0, G):
            for t0 in range(0, G, 128):
                norm(c0 + t0, 128, 4 + l, xnq, rxq, sq, rsq, rs, rrs, 5, 7)
                yield
                for m in range(16):
                    ub, rub = stream_u(wqbf[l, m], rcQ[l][m // 8])
                    bk, rb = nb(5, 7)
                    for kc in range(8):
                        fw.op('pe', MM(bk[:, 0:128], ub[:, kc, :], xnq[:, kc, :], kc == 0, kc == 7), reads=[rub, rxq], writes=[rb], inc=(kc == 7))
                    fw.op('act', ACP(qT[:, m, :], bk[:, 0:128]), reads=[rb], writes=[rq])
                    yield
                for hf in range(2):
                    bks = [nb(5, 7) for _ in range(2)]
                    for cc in range(8):
                        c = hf * 8 + cc
                        bk, rb = bks[cc // 4]
                        fw.op('pe', MM(bk[:, (cc % 4) * 128:(cc % 4 + 1) * 128], qT[:, c, :], skt[:, c, :], True, True),
                              reads=[rq, rSK], writes=[rb], inc=(cc % 4 == 3))
                    for k2 in range(2):
                        bk, rb = bks[k2]
                        o = hf * 1024 + k2 * 512
                        fw.op('act', ACP(big[:, o:o + 512], bk[:, :]), reads=[rb], writes=[rbig])
                    yield
                for c in range(16):
                    scc = big[:, c * 128:(c + 1) * 128]
                    fw.op('dve', MAX8(sv[:, c, 0:8], scc), reads=[rbig], writes=[rsv])
                    fw.op('dve', MIDX(si[:, c, 0:8], sv[:, c, 0:8], scc), reads=[rbig, rsv], writes=[rsi])
                    fw.op('dve', MREP(tm[:, 0:128], sv[:, c, 0:8], scc), reads=[rbig, rsv], writes=[rtm])
                    fw.op('dve', MAX8(sv[:, c, 8:16], tm[:, 0:128]), reads=[rtm], writes=[rsv])
                    fw.op('dve', MIDX(si[:, c, 8:16], sv[:, c, 8:16], tm[:, 0:128]), reads=[rtm, rsv], writes=[rsi])
                    yield
                fw.op('dve', TC(sif, si), reads=[rsi], writes=[rsif])
                svv = sv.rearrange("p (h two) a -> p h two a", two=2)
                sfv = sif.rearrange("p (h two) a -> p h two a", two=2)
                cand = big.rearrange("p (h a b) -> p h a b", h=8, a=16)
                cand2 = big.rearrange("p (h x) -> p h x", h=8)
                fw.op('dve', TT(cand, svv[:, :, 0, :].unsqueeze(3).to_broadcast([128, 8, 16, 16]),
                                svv[:, :, 1, :].unsqueeze(2).to_broadcast([128, 8, 16, 16]), ALU.add), reads=[rsv], writes=[rbig])
                yield
                for h in range(8):
                    fw.op('dve', MAX8(best[:, h, 0:8], cand2[:, h, :]), reads=[rbig], writes=[rbest])
                    fw.op('dve', MIDX(pos[:, h, 0:8], best[:, h, 0:8], cand2[:, h, :]), reads=[rbig, rbest], writes=[rpos])
                    fw.op('dve', MREP(tm, best[:, h, 0:8], cand2[:, h, :]), reads=[rbig, rbest], writes=[rtm])
                    fw.op('dve', MAX8(best[:, h, 8:16], tm), reads=[rtm], writes=[rbest])
                    fw.op('dve', MIDX(pos[:, h, 8:16], best[:, h, 8:16], tm), reads=[rtm, rbest], writes=[rpos])
                    yield
                fw.op('dve', TC(posf, pos), reads=[rpos], writes=[rpp])
                ge = big[:, 0:1920].rearrange("p (x m) -> p x m", m=15)
                pfl = posf.rearrange("p h k -> p (h k)")
                fw.op('dve', TT(ge, pfl.unsqueeze(2).to_broadcast([128, 128, 15]), thr16[:, 0:15].unsqueeze(1).to_broadcast([128, 128, 15]), ALU.is_ge),
                      reads=[rpp, rC], writes=[rbig])
                fw.op('dve', RED(paf.rearrange("p h k -> p (h k)"), ge, ALU.add), reads=[rbig], writes=[rpp])
                fw.op('dve', STT(pbf, paf, -16.0, posf, ALU.mult, ALU.add), reads=[rpp], writes=[rpp])
                yield
                eq = big.rearrange("p (h k a) -> p h k a", h=8, k=16)
                io = iota16[:, :].unsqueeze(1).unsqueeze(1).to_broadcast([128, 8, 16, 16])
                for which, pf in ((0, paf), (1, pbf)):
                    fw.op('dve', TT(eq, pf.unsqueeze(3).to_broadcast([128, 8, 16, 16]), io, ALU.is_equal), reads=[rpp, rC], writes=[rbig])
                    fw.op('dve', TT(eq, eq, sfv[:, :, which, :].unsqueeze(2).to_broadcast([128, 8, 16, 16]), ALU.mult), reads=[rbig, rsif], writes=[rbig])
                    fw.op('dve', RED(sel[:, which, :, :], eq, ALU.add), reads=[rbig], writes=[rsel])
                    yield
                gx = sel[:, 2, :, :]
                fw.op('dve', TT(gx, best, best[:, :, 0:1].to_broadcast([128, 8, 16]), ALU.subtract), reads=[rbest], writes=[rsel])
                fw.op('act', ACTF(gx, gx, AF.Exp), reads=[rsel], writes=[rsel])
                fw.op('dve', RED(gs, gx, ALU.add), reads=[rsel], writes=[rgs])
                fw.op('dve', RCP(gs, gs), reads=[rgs], writes=[rgs])
                fw.op('dve', TT(gx, gx, gs.unsqueeze(2).to_broadcast([128, 8, 16]), ALU.mult), reads=[rsel, rgs], writes=[rsel])
                yield
                bk, rb = nb(5, 7)
                for w3 in range(3):
                    fw.op('pe', TRN(bk[:, w3 * 128:(w3 + 1) * 128], sel[:, w3, :, :].rearrange("p h k -> p (h k)"), ident_f[:]),
                          reads=[rsel, rC], writes=[rb], inc=(w3 == 2))
                fw.op('act', ACP(trio[:, :, t0:t0 + 128], bk[:, 0:384].rearrange("p (w t) -> p w t", w=3)), reads=[rb], writes=[rtrio])
                yield

        def expert_group(c0, G, nxt):
            for t4 in range(0, G, 4):
                bk, rb = nb(5, 7)
                for tt in range(4):
                    t = t4 + tt
                    s = octr[0] % NO
                    octr[0] += 1
                    fw.op('dve', TS(O2[s], iota_b[:], trio[:, 1, t:t + 1], None, ALU.is_equal), reads=[rtrio, rC], writes=[rO2[s]])
                    fw.op('dve', TS(O1[s], iota_b[:], trio[:, 0, t:t + 1], trio[:, 2, t:t + 1], ALU.is_equal, ALU.mult), reads=[rtrio, rC], writes=[rO1[s]])
                    fw.op('pe', MM(bk[:, tt * 128:(tt + 1) * 128], O2[s], O1[s], True, True), reads=[rO2[s], rO1[s]], writes=[rb], inc=True)
                fw.op('act', ACP(Wv[:, :, t4:t4 + 4], bk[:, :].rearrange("p (t i) -> p i t", t=4)), reads=[rb], writes=[rW])
            norm(c0, G, 4 + l, xn, rxn, sq, rsq, rs, rrs, 5, 7)
            NTT = G // 128
            ybanks = [(banks[q], rB[q]) for q in range(2 * NTT)]

            def emit_y(i, a2, vb, rvb):
                for tt in range(NTT):
                    for dh in range(2):
                        yb, ryb = ybanks[tt * 2 + dh]
                        fw.op('pe', MM(yb[:, :], wa[a2][:, tt * 128:(tt + 1) * 128], vb[:, dh * 512:(dh + 1) * 512], i == 0, i == 127),
                              reads=[rvb, rwa[a2]], writes=[ryb], inc=(tt == NTT - 1 and dh == 1))

            prev = None
            for i in range(128):
                ub, rub = stream_u(ubf[l, i], rcU[l][i // 8])
                vb, rvb = stream_v(vbf[l, i], rcV[l][i // 8])
                a2 = i % 2
                ab = banks[4] if a2 == 0 else banks[7]
                for kc in range(8):
                    fw.op('pe', MM(ab[:, 0:G], ub[:, kc, :], xn[:, kc, 0:G], kc == 0, kc == 7), reads=[rub, rxn], writes=[rAB[a2]], inc=(kc == 7))
                fw.op('act', ACTF(asb[a2][:, 0:G], ab[:, 0:G], AF.Gelu), reads=[rAB[a2]], writes=[rasb[a2]])
                fw.op('dve', TT(wa[a2][:, 0:G], asb[a2][:, 0:G], Wv[:, i, 0:G], ALU.mult), reads=[rasb[a2], rW], writes=[rwa[a2]])
                if prev is not None:
                    emit_y(*prev)
                prev = (i, a2, vb, rvb)
                if nxt is not None and i >= 4:
                    next(nxt, None)
            emit_y(*prev)
            if nxt is not None:
                for _ in nxt:
                    pass
            for tt in range(NTT):
                for dh in range(2):
                    yb, ryb = ybanks[tt * 2 + dh]
                    fw.op('act', ACP(ysb[:, tt, dh * 512:(dh + 1) * 512], yb[:, :]), reads=[ryb], writes=[rW])
                for half in range(2):
                    bk, rb = nb(5, 7)
                    for k4 in range(4):
                        m = half * 4 + k4
                        fw.op('pe', TRN(bk[:, k4 * 128:(k4 + 1) * 128], ysb[:, tt, m * 128:(m + 1) * 128], ident_f[:]), reads=[rW, rC], writes=[rb], inc=(k4 == 3))
                    for k4 in range(4):
                        m = half * 4 + k4
                        resid_add(c0 + tt * 128, 128, m, bk[:, k4 * 128:(k4 + 1) * 128], rb)

        groups = []
        c0 = c_peer
        while c0 < NT:
            G = min(256, NT - c0)
            groups.append((c0, G))
            c0 += G
        for _ in route_gen(*groups[0]):
            pass
        for gi, (c0, G) in enumerate(groups):
            nxt = route_gen(*groups[gi + 1]) if gi + 1 < len(groups) else None
            expert_group(c0, G, nxt)
            issue_conv(l + 1, 4)
        issue_conv(l + 1, 1000)
        fw.barrier()

    def final():
        A.reset()
        xn = A.get([8, 128], F32); rxn = Res()
        sq = A.get([8, 128], BF16); rsq = Res()
        rs = A.get([128], F32); rrs = Res()
        ob = [A.get([1024], F32) for _ in range(2)]; rob = [Res() for _ in range(2)]
        n = 0
        for c0 in list(range(OWN0, SMP0, 128)) + [SMP0]:
            norm(c0, 128, 8, xn, rxn, sq, rsq, rs, rrs)
            o = ob[n % 2]
            ro = rob[n % 2]
            n += 1
            for half in range(2):
                bk, rb = nb()
                for k4 in range(4):
                    fw.op('pe', TRN(bk[:, k4 * 128:(k4 + 1) * 128], xn[:, half * 4 + k4, :], ident_f[:]), reads=[rxn, rC], writes=[rb], inc=(k4 == 3))
                fw.op('act', ACP(o[:, half * 512:(half + 1) * 512], bk[:, :]), reads=[rb], writes=[ro])
            if c0 < SMP0:
                fw.dma('sp', yp_d[c0 - OWN0:c0 - OWN0 + 128, :], o, reads=[ro])
            else:
                fw.dma('sp', ys_d[:, :], o[0:64, :], reads=[ro])

    C_KV = [0, None, 256, None]
    C_MIX = [None, 128, None, 384]
    C_PEER = [128, 256, 384, 512]
    for l in range(n_layers):
        j = l // 2
        if l % 2 == 0:
            attn_layer(l, j, C_KV[l])
        else:
            conv_layer(l, j, C_MIX[l])
        if do_peer:
            peer_layer(l, C_PEER[l])
    final()
    fw.barrier()
    fw.emit()
    print("instr counts", {e: len(fw.q[e]) for e in ENGS})
    return nc


def _t5_bucket(dist):
    n = np.maximum(dist, 0)
    nf = np.maximum(n, 1).astype(np.float32)
    large = 16 + (np.log(nf / 16) / np.float32(np.log(128 / 16)) * 16).astype(np.int32)
    large = np.minimum(large, 31)
    return np.where(n < 16, n, large)


def _bucket_onehot():
    c = np.arange(128)[:, None]
    q = np.arange(128)[None, :]
    out = np.zeros((128, 33, 2, 128), np.float32)
    for w, dist in ((0, q - c + 128), (1, q - c)):
        valid = (dist >= 0) & (dist < 128)
        bk = _t5_bucket(dist)
        for b in range(32):
            out[:, b, w, :] = (valid & (bk == b)).astype(np.float32)
        out[:, 32, w, :] = (~valid).astype(np.float32)
    return out


def prep_inputs(x_prompt, x_sample, cache_k, cache_v, state_conv, norm_mix_g, norm_ffn_g, norm_final_g,
                rel_bias, attn_w_qkv, attn_sinks, attn_w_o, conv_w_in, conv_w, conv_w_out,
                peer_w_q, peer_sub_keys, peer_u, peer_v):
    f = lambda a: np.ascontiguousarray(np.asarray(a, dtype=np.float32))
    x_prompt, x_sample = f(x_prompt), f(x_sample)
    cache_k, cache_v, state_conv = f(cache_k), f(cache_v), f(state_conv)
    sh = {}
    g = np.concatenate([f(norm_mix_g), f(norm_ffn_g), f(norm_final_g)[None]], 0)
    sh["gains"] = f(g.reshape(9, 8, 128).transpose(2, 0, 1).reshape(128, 72))
    relb = np.concatenate([f(rel_bias), np.full((1, 16), NEG, np.float32)], 0).reshape(1, 33 * 16)
    sh["relb"] = f(np.repeat(relb, 128, 0))
    sh["ohb"] = f(_bucket_onehot().reshape(128, 8, 1056))
    sh["sinks"] = f(np.repeat(f(attn_sinks).reshape(1, 32), 64, 0))
    sh["wqkv"] = f(f(attn_w_qkv).reshape(2, 8, 128, 1536).transpose(0, 2, 1, 3))
    sh["wo"] = f(f(attn_w_o).reshape(2, 16, 64, 1024).transpose(0, 2, 1, 3))
    sh["win"] = f(f(conv_w_in).reshape(2, 8, 128, 3072).transpose(0, 2, 1, 3).reshape(2, 128, 24, 1024))
    sh["wout"] = f(f(conv_w_out).reshape(2, 8, 128, 1024).transpose(0, 2, 1, 3))
    sh["convw"] = f(f(conv_w).reshape(2, 3, 8, 128).transpose(3, 0, 1, 2))
    sh["wq"] = f(f(peer_w_q).reshape(4, 8, 128, 16, 128).transpose(0, 3, 2, 1, 4).reshape(4, 16, 128, 1024))
    sh["skt"] = f(f(peer_sub_keys).reshape(4, 16, 128, 128).transpose(0, 3, 1, 2).reshape(4, 128, 2048))
    sh["ut"] = f(f(peer_u).reshape(4, 128, 128, 8, 128).transpose(0, 1, 4, 3, 2).reshape(4, 128, 128, 1024))
    sh["vt"] = f(f(peer_v).reshape(4, 128, 128, 1024))
    maps = []
    for c in range(8):
        b, half = c // 2, c % 2
        s0 = half * 2048
        cols = np.zeros((NT, 1024), np.float32)
        if half == 1:
            cols[0:512] = x_prompt[b, s0 - 512:s0]
        cols[512:2560] = x_prompt[b, s0:s0 + 2048]
        cols[2560:2624] = x_sample[16 * c:16 * c + 16].reshape(64, 1024)
        m = dict(sh)
        m["xT"] = f(cols.reshape(NT, 8, 128).transpose(2, 1, 0))
        m["flag"] = np.full((128, 1), NEG if half == 0 else 0.0, np.float32)
        m["ck"] = f(cache_k[:, 16 * c:16 * c + 16].reshape(2, 16, 128, 256))
        m["cv"] = f(cache_v[:, 16 * c:16 * c + 16].reshape(2, 16, 128, 256))
        m["stT"] = f(state_conv[:, 16 * c:16 * c + 16].reshape(2, 16, 2, 8, 128).transpose(4, 0, 3, 1, 2))
        maps.append(m)
    return maps


def assemble(results):
    yp = np.zeros((4, 4096, 1024), np.float32)
    ys = np.zeros((128, 4, 1024), np.float32)
    kp = np.zeros((2, 4, 128, 4, 64), np.float32)
    vp = np.zeros((2, 4, 128, 4, 64), np.float32)
    cp = np.zeros((2, 4, 2, 1024), np.float32)
    ksn = np.zeros((2, 128, 128, 4, 64), np.float32)
    vsn = np.zeros((2, 128, 128, 4, 64), np.float32)
    csn = np.zeros((2, 128, 2, 1024), np.float32)
    for c in range(8):
        r = results[c]
        b, half = c // 2, c % 2
        yp[b, half * 2048:(half + 1) * 2048] = r["yp"]
        ys[16 * c:16 * c + 16] = r["ysm"].reshape(16, 4, 1024)
        if half == 1:
            kp[:, b] = r["kvp"][:, :, 0:256].reshape(2, 128, 4, 64)
            vp[:, b] = r["kvp"][:, :, 256:512].reshape(2, 128, 4, 64)
            cp[:, b] = r["cp"]
        kw = np.concatenate([r["ksn"][:, :, 4:128, :], r["knw"][:, :, :, 0:256]], axis=2)
        vw = np.concatenate([r["vsn"][:, :, 4:128, :], r["knw"][:, :, :, 256:512]], axis=2)
        ksn[:, 16 * c:16 * c + 16] = kw.reshape(2, 16, 128, 4, 64)
        vsn[:, 16 * c:16 * c + 16] = vw.reshape(2, 16, 128, 4, 64)
        csn[:, 16 * c:16 * c + 16] = r["csn"].reshape(2, 16, 2, 1024)
    return (yp, ys, kp, vp, cp, ksn, vsn, csn)


def kernel(**inputs):
    maps = prep_inputs(**inputs)
    nc = build_program()
    res = run_bass_kernel_spmd(nc, maps, core_ids=list(range(8)))
    return assemble(res.results)
```
